# Optimizing a Trainium2 kernel written in Bass

```python
import math
import jax, jax.numpy as jnp
from jax import lax
import numpy as np

D_MODEL = 1024
BATCH = 8
SEQ = 4096
DEPTH = 2

GRID_W = 64
CTX_LEN = 256
EPS = 1e-6
ROPE_THETA = 10000.0
F32 = jnp.float32

A_HEADS = 4
A_QK_DIM = 64
A_V_DIM = 2 * A_QK_DIM
A_WIDTH = A_HEADS * A_V_DIM
A_PROJ = 2 * (A_HEADS * 2 * A_QK_DIM) + A_WIDTH
Q_BLOCK = 128

B_HEADS = 8
B_HEAD = 64
B_WIDTH = B_HEADS * B_HEAD
B_DECAY_LORA = 64
B_AAA_LORA = 64
B_GATE_LORA = 128
B_GN_EPS = 64e-5
B_PROJ = 3 * B_WIDTH + B_DECAY_LORA + B_AAA_LORA + B_GATE_LORA

C_HEADS = 8
C_HEAD = 64
C_WIDTH = C_HEADS * C_HEAD
C_GROUPS = 2
C_REP = C_HEADS // C_GROUPS
C_STATE = 128
C_CONV = 5
C_CHUNK = 128
C_CONV_DIM = C_WIDTH + 2 * C_GROUPS * C_STATE
C_PROJ = C_WIDTH + C_CONV_DIM + 2 * C_HEADS

D_HEADS = 4
D_KEY = 64
D_VAL = 128
D_WIDTH = D_HEADS * D_VAL
D_GATE_LORA = 16
D_GATE_TAU = 16.0
D_CHUNK = 64
D_PROJ = 2 * D_HEADS * D_KEY + 2 * D_WIDTH + D_GATE_LORA

MLP_HIDDEN = 4 * D_MODEL
EVEN_IN = A_PROJ + B_PROJ
ODD_IN = C_PROJ + D_PROJ
MIX_WIDTH = A_WIDTH + B_WIDTH

kernel_name = 'hybrid_diffattn_rwkv7_ssd_gla_dit'


def rmsnorm(x, g, eps=EPS):
    xf = x.astype(F32)
    y = xf * lax.rsqrt(jnp.mean(xf * xf, axis=-1, keepdims=True) + eps)
    return (y * g.astype(F32)).astype(x.dtype)


def _split(t, sizes):
    idx = [int(s) for s in np.cumsum(sizes)[:-1]]
    return jnp.split(t, idx, axis=-1)


def _flip_seq(*ts):
    return tuple(jnp.flip(t, axis=1) for t in ts)


def axial_rope_tables(n, dim):
    rows = n // GRID_W
    t = jnp.arange(rows * GRID_W)
    row = (t // GRID_W).astype(F32)
    col = (t % GRID_W).astype(F32)
    nf = dim // 4
    inv = ROPE_THETA ** (-jnp.arange(nf, dtype=F32) / nf)
    ar = row[:, None] * inv[None, :]
    ac = col[:, None] * inv[None, :]
    return jnp.cos(ar), jnp.sin(ar), jnp.cos(ac), jnp.sin(ac)


def _rot_half(x, cos, sin):
    x1, x2 = jnp.split(x, 2, axis=-1)
    return jnp.concatenate([x1 * cos - x2 * sin, x2 * cos + x1 * sin], axis=-1)


def apply_axial_rope(x, tabs):
    cr, sr, cc, sc = tabs
    xr, xc = jnp.split(x.astype(F32), 2, axis=-1)
    return jnp.concatenate([_rot_half(xr, cr, sr), _rot_half(xc, cc, sc)], axis=-1)


def _diff_softmax_mix(q, k, v, lam):
    s = jnp.einsum('bhmqd,bhmkd->bhmqk', q.astype(F32), k.astype(F32)) * (A_QK_DIM ** -0.5)
    p = jax.nn.softmax(s, axis=-1)
    w = p[:, :, 0] - lam * p[:, :, 1]
    return jnp.einsum('bhqk,bhkd->bhqd', w, v.astype(F32))


def diff_attention(t_lat, t_ctx, lam_q1, lam_k1, lam_q2, lam_k2, subln, layer_idx, need_ctx):
    bsz, n = t_lat.shape[0], t_lat.shape[1]
    lam_init = 0.8 - 0.6 * math.exp(-0.3 * layer_idx)
    lam = (jnp.exp(jnp.sum(lam_q1.astype(F32) * lam_k1.astype(F32)))
           - jnp.exp(jnp.sum(lam_q2.astype(F32) * lam_k2.astype(F32))) + lam_init)

    def heads(t):
        m = t.shape[1]
        q, k, v = _split(t, [A_HEADS * 2 * A_QK_DIM, A_HEADS * 2 * A_QK_DIM, A_WIDTH])
        q = q.reshape(bsz, m, A_HEADS, 2, A_QK_DIM).transpose(0, 2, 3, 1, 4)
        k = k.reshape(bsz, m, A_HEADS, 2, A_QK_DIM).transpose(0, 2, 3, 1, 4)
        v = v.reshape(bsz, m, A_HEADS, A_V_DIM).transpose(0, 2, 1, 3)
        return q, k, v

    q_l, k_l, v_l = heads(t_lat)
    q_c, k_c, v_c = heads(t_ctx)
    tabs = axial_rope_tables(n, A_QK_DIM)
    q_l = apply_axial_rope(q_l, tabs)
    k_l = apply_axial_rope(k_l, tabs)
    k_all = jnp.concatenate([k_c.astype(F32), k_l], axis=3)
    v_all = jnp.concatenate([v_c, v_l], axis=2)
    nb = n // Q_BLOCK
    qb = jnp.moveaxis(q_l.reshape(bsz, A_HEADS, 2, nb, Q_BLOCK, A_QK_DIM), 3, 0)
    ob = lax.map(lambda qq: _diff_softmax_mix(qq, k_all, v_all, lam), qb)
    o_l = jnp.moveaxis(ob, 0, 2).reshape(bsz, A_HEADS, n, A_V_DIM)

    def post(o):
        o = rmsnorm(o, subln, eps=1e-5) * (1.0 - lam_init)
        return o.transpose(0, 2, 1, 3).reshape(bsz, o.shape[2], A_WIDTH)

    out_l = post(o_l)
    out_c = post(_diff_softmax_mix(q_c, k_c, v_c, lam)) if need_ctx else None
    return out_l, out_c


def _token_shift_mix(f, mu):
    prev = jnp.pad(f[:, :-1], ((0, 0), (1, 0), (0, 0)))
    nxt = jnp.pad(f[:, 1:], ((0, 0), (0, 1), (0, 0)))
    return f + (0.5 * (prev + nxt) - f) * mu


def _rwkv7_scan(S0, r, w, k, v, kk, a):
    def step(S, inp):
        r_t, w_t, k_t, v_t, kk_t, a_t = inp
        sa = jnp.einsum('bhvk,bhk->bhv', S, kk_t)
        S = (S * w_t[:, :, None, :] - sa[..., None] * (kk_t * a_t)[:, :, None, :]
             + v_t[..., None] * k_t[:, :, None, :])
        return S, jnp.einsum('bhvk,bhk->bhv', S, r_t)
    xs = tuple(jnp.swapaxes(t, 0, 1) for t in (r, w, k, v, kk, a))
    S, ys = lax.scan(step, S0, xs)
    return jnp.swapaxes(ys, 0, 1), S


def rwkv7_mix(t_lat, t_ctx, mu, w0_f, w2_f, w0_b, w2_b, a0, a2, g2, k_k, k_a, r_k, lnx_w, lnx_b, need_ctx):
    def prep(t):
        bsz, n = t.shape[0], t.shape[1]
        f = _token_shift_mix(t, mu).astype(F32)
        r, k, v, wd, ad, gd = _split(f, [B_WIDTH, B_WIDTH, B_WIDTH, B_DECAY_LORA, B_AAA_LORA, B_GATE_LORA])
        hd = lambda u: u.reshape(bsz, n, B_HEADS, B_HEAD)
        a = jax.nn.sigmoid(a0 + ad @ a2)
        tw = jnp.tanh(wd)
        def decay(w0, w2):
            wl = -jax.nn.softplus(-(w0 + tw @ w2)) - 0.5
            return hd(jnp.exp(-jnp.exp(wl)))
        kk = hd(k * k_k)
        kk = kk / jnp.maximum(jnp.sqrt(jnp.sum(kk * kk, axis=-1, keepdims=True)), 1e-12)
        k = k * (1.0 + (a - 1.0) * k_a)
        g = jax.nn.sigmoid(gd) @ g2
        return (hd(r), hd(k), hd(v), kk, hd(a), decay(w0_f, w2_f), decay(w0_b, w2_b), g)

    r_c, k_c, v_c, kk_c, a_c, wf_c, wb_c, g_c = prep(t_ctx)
    r_l, k_l, v_l, kk_l, a_l, wf_l, wb_l, g_l = prep(t_lat)
    S0 = jnp.zeros((t_lat.shape[0], B_HEADS, B_HEAD, B_HEAD), F32)
    yf_c, Sf_c = _rwkv7_scan(S0, r_c, wf_c, k_c, v_c, kk_c, a_c)
    yb_c, Sb_c = _rwkv7_scan(S0, *_flip_seq(r_c, wb_c, k_c, v_c, kk_c, a_c))
    yf_l, _ = _rwkv7_scan(Sf_c, r_l, wf_l, k_l, v_l, kk_l, a_l)
    yb_l, _ = _rwkv7_scan(Sb_c, *_flip_seq(r_l, wb_l, k_l, v_l, kk_l, a_l))

    def post(r, k, v, g, yf, yb):
        bsz, n = r.shape[0], r.shape[1]
        y = yf + jnp.flip(yb, axis=1)
        mean = jnp.mean(y, axis=-1, keepdims=True)
        var = jnp.mean(jnp.square(y - mean), axis=-1, keepdims=True)
        y = ((y - mean) * lax.rsqrt(var + B_GN_EPS)).reshape(bsz, n, B_WIDTH) * lnx_w + lnx_b
        bonus = jnp.sum(r * k * r_k.reshape(B_HEADS, B_HEAD), axis=-1, keepdims=True) * v
        return (y + bonus.reshape(bsz, n, B_WIDTH)) * g

    out_l = post(r_l, k_l, v_l, g_l, yf_l, yb_l)
    out_c = post(r_c, k_c, v_c, g_c, yf_c, yb_c) if need_ctx else None
    return out_l, out_c


def _dwconv_centred(t, w, b):
    y = lax.conv_general_dilated(t, w[:, None, :].astype(t.dtype), window_strides=(1,),
                                 padding=[(C_CONV // 2, C_CONV // 2)],
                                 dimension_numbers=('NWC', 'WIO', 'NWC'),
                                 feature_group_count=t.shape[-1])
    return y + b


def _ssd_chunk_scan(h0, x, dt, la, bm, cm):
    bsz, n = x.shape[0], x.shape[1]
    nc = n // C_CHUNK
    chunks = lambda t: jnp.swapaxes(t.reshape((bsz, nc, C_CHUNK) + t.shape[2:]), 0, 1)
    idx = jnp.arange(C_CHUNK)
    lower = (idx[:, None] >= idx[None, :])[None, :, :, None, None]

    def step(h, inp):
        xc, dtc, lac, bc, cc = inp
        cum = jnp.cumsum(lac, axis=1)
        lmat = jnp.exp(jnp.where(lower, cum[:, :, None] - cum[:, None, :], -jnp.inf))
        cb = jnp.einsum('bign,bjgn->bijg', cc, bc)
        xdt = xc * dtc[..., None]
        y = jnp.einsum('bijgr,bjgrp->bigrp', cb[..., None] * lmat, xdt)
        y = y + jnp.einsum('bign,bgrpn->bigrp', cc, h) * jnp.exp(cum)[..., None]
        w_end = jnp.exp(cum[:, -1:] - cum)
        h = h * jnp.exp(cum[:, -1])[..., None, None] + jnp.einsum('bjgn,bjgrp->bgrpn', bc, xdt * w_end[..., None])
        return h, y

    h, ys = lax.scan(step, h0, tuple(chunks(t) for t in (x, dt, la, bm, cm)))
    return jnp.swapaxes(ys, 0, 1).reshape(x.shape), h


def ssd_mix(t_lat, t_ctx, conv_w, conv_b, dt_bias_f, a_log_f, dt_bias_b, a_log_b, d_skip, norm_g, need_ctx):
    def prep(t):
        bsz, n = t.shape[0], t.shape[1]
        z, xbc, dt_f, dt_b = _split(t, [C_WIDTH, C_CONV_DIM, C_HEADS, C_HEADS])
        xbc = jax.nn.silu(_dwconv_centred(xbc, conv_w, conv_b).astype(F32))
        xs, bm, cm = _split(xbc, [C_WIDTH, C_GROUPS * C_STATE, C_GROUPS * C_STATE])
        xs = xs.reshape(bsz, n, C_GROUPS, C_REP, C_HEAD)
        bm = bm.reshape(bsz, n, C_GROUPS, C_STATE)
        cm = cm.reshape(bsz, n, C_GROUPS, C_STATE)
        def direction(dt_raw, dt_bias, a_log):
            dt = jax.nn.softplus(dt_raw.astype(F32) + dt_bias).reshape(bsz, n, C_GROUPS, C_REP)
            return dt, dt * (-jnp.exp(a_log.astype(F32))).reshape(C_GROUPS, C_REP)
        return z, xs, bm, cm, direction(dt_f, dt_bias_f, a_log_f), direction(dt_b, dt_bias_b, a_log_b)

    z_c, x_c, b_c, c_c, (dtf_c, laf_c), (dtb_c, lab_c) = prep(t_ctx)
    z_l, x_l, b_l, c_l, (dtf_l, laf_l), (dtb_l, lab_l) = prep(t_lat)
    h0 = jnp.zeros((t_lat.shape[0], C_GROUPS, C_REP, C_HEAD, C_STATE), F32)
    yf_c, hf_c = _ssd_chunk_scan(h0, x_c, dtf_c, laf_c, b_c, c_c)
    yb_c, hb_c = _ssd_chunk_scan(h0, *_flip_seq(x_c, dtb_c, lab_c, b_c, c_c))
    yf_l, _ = _ssd_chunk_scan(hf_c, x_l, dtf_l, laf_l, b_l, c_l)
    yb_l, _ = _ssd_chunk_scan(hb_c, *_flip_seq(x_l, dtb_l, lab_l, b_l, c_l))

    def post(z, xs, yf, yb):
        bsz, n = xs.shape[0], xs.shape[1]
        y = yf + jnp.flip(yb, axis=1) + xs * d_skip.astype(F32).reshape(C_GROUPS, C_REP, 1)
        y = y.reshape(bsz, n, C_WIDTH) * jax.nn.silu(z.astype(F32))
        y = rmsnorm(y.reshape(bsz, n, C_GROUPS, C_WIDTH // C_GROUPS), norm_g.reshape(C_GROUPS, C_WIDTH // C_GROUPS))
        return y.reshape(bsz, n, C_WIDTH)

    out_l = post(z_l, x_l, yf_l, yb_l)
    out_c = post(z_c, x_c, yf_c, yb_c) if need_ctx else None
    return out_l, out_c


def _gla_chunk_scan(S0, q, k, v, lg):
    bsz, n = q.shape[0], q.shape[1]
    nc = n // D_CHUNK
    chunks = lambda t: jnp.swapaxes(t.reshape((bsz, nc, D_CHUNK) + t.shape[2:]), 0, 1)
    idx = jnp.arange(D_CHUNK)
    lower = (idx[:, None] >= idx[None, :])[None, :, :, None, None]

    def step(S, inp):
        qc, kc, vc, gc = inp
        cum = jnp.cumsum(gc, axis=1)
        dec = jnp.exp(jnp.where(lower, cum[:, :, None] - cum[:, None, :], -jnp.inf))
        att = jnp.einsum('bihk,bjhk,bijhk->bhij', qc, kc, dec)
        y = jnp.einsum('bhij,bjhv->bihv', att, vc) + jnp.einsum('bihk,bhkv->bihv', qc * jnp.exp(cum), S)
        S = S * jnp.exp(cum[:, -1])[..., None] + jnp.einsum('bjhk,bjhv->bhkv', kc * jnp.exp(cum[:, -1:] - cum), vc)
        return S, y

    S, ys = lax.scan(step, S0, tuple(chunks(t) for t in (q, k, v, lg)))
    return jnp.swapaxes(ys, 0, 1).reshape(v.shape), S


def gla_mix(t_lat, t_ctx, gk_up_f, gk_b_f, gk_up_b, gk_b_b, norm_g, need_ctx):
    def prep(t):
        bsz, n = t.shape[0], t.shape[1]
        q, k, v, g, gd = _split(t.astype(F32), [D_HEADS * D_KEY, D_HEADS * D_KEY, D_WIDTH, D_WIDTH, D_GATE_LORA])
        hk = lambda u: u.reshape(bsz, n, D_HEADS, D_KEY)
        def log_gate(up, b):
            return hk(jax.nn.log_sigmoid(gd @ up + b) / D_GATE_TAU)
        return (hk(q) * (D_KEY ** -0.5), hk(k), v.reshape(bsz, n, D_HEADS, D_VAL), g,
                log_gate(gk_up_f, gk_b_f), log_gate(gk_up_b, gk_b_b))

    q_c, k_c, v_c, g_c, gf_c, gb_c = prep(t_ctx)
    q_l, k_l, v_l, g_l, gf_l, gb_l = prep(t_lat)
    S0 = jnp.zeros((t_lat.shape[0], D_HEADS, D_KEY, D_VAL), F32)
    yf_c, Sf_c = _gla_chunk_scan(S0, q_c, k_c, v_c, gf_c)
    yb_c, Sb_c = _gla_chunk_scan(S0, *_flip_seq(q_c, k_c, v_c, gb_c))
    yf_l, _ = _gla_chunk_scan(Sf_c, q_l, k_l, v_l, gf_l)
    yb_l, _ = _gla_chunk_scan(Sb_c, *_flip_seq(q_l, k_l, v_l, gb_l))

    def post(g, yf, yb):
        bsz, n = g.shape[0], g.shape[1]
        y = rmsnorm(yf + jnp.flip(yb, axis=1), norm_g)
        return y.reshape(bsz, n, D_WIDTH) * jax.nn.silu(g)

    out_l = post(g_l, yf_l, yb_l)
    out_c = post(g_c, yf_c, yb_c) if need_ctx else None
    return out_l, out_c


def even_mixer(p_lat, p_ctx, lam_q1, lam_k1, lam_q2, lam_k2, subln, mu, w0_f, w2_f, w0_b, w2_b,
               a0, a2, g2, k_k, k_a, r_k, lnx_w, lnx_b, layer_idx, need_ctx):
    oa_l, oa_c = diff_attention(p_lat[..., :A_PROJ], p_ctx[..., :A_PROJ],
                                lam_q1, lam_k1, lam_q2, lam_k2, subln, layer_idx, need_ctx)
    ob_l, ob_c = rwkv7_mix(p_lat[..., A_PROJ:], p_ctx[..., A_PROJ:], mu, w0_f, w2_f, w0_b, w2_b,
                           a0, a2, g2, k_k, k_a, r_k, lnx_w, lnx_b, need_ctx)
    o_l = jnp.concatenate([oa_l, ob_l], axis=-1)
    o_c = jnp.concatenate([oa_c, ob_c], axis=-1) if need_ctx else None
    return o_l, o_c


def odd_mixer(p_lat, p_ctx, conv_w, conv_b, dt_bias_f, a_log_f, dt_bias_b, a_log_b, d_skip, ssm_norm,
              gk_up_f, gk_b_f, gk_up_b, gk_b_b, gla_norm, need_ctx):
    oc_l, oc_c = ssd_mix(p_lat[..., :C_PROJ], p_ctx[..., :C_PROJ], conv_w, conv_b, dt_bias_f, a_log_f,
                         dt_bias_b, a_log_b, d_skip, ssm_norm, need_ctx)
    od_l, od_c = gla_mix(p_lat[..., C_PROJ:], p_ctx[..., C_PROJ:], gk_up_f, gk_b_f, gk_up_b, gk_b_b,
                         gla_norm, need_ctx)
    o_l = jnp.concatenate([oc_l, od_l], axis=-1)
    o_c = jnp.concatenate([oc_c, od_c], axis=-1) if need_ctx else None
    return o_l, o_c


def _ada(x, g, shift, scale):
    return rmsnorm(x, g) * (1.0 + scale) + shift


def _sq_relu_mlp(h, w1, w2):
    return jnp.square(jax.nn.relu(h @ w1)) @ w2


def setup_inputs(seed: int = 0) -> dict:
    key = jax.random.key(seed)
    keys = iter(jax.random.split(key, 64))
    D = D_MODEL
    def nrm(shape, scale):
        return jax.random.normal(next(keys), shape, F32) * scale
    def gain(n):
        return 1.0 + 0.05 * jax.random.normal(next(keys), (n,), F32)
    def unif(shape, lo, hi):
        return jax.random.uniform(next(keys), shape, F32, lo, hi)
    def dt_bias(n):
        dt = jnp.exp(unif((n,), math.log(1e-3), math.log(1e-1)))
        return dt + jnp.log(-jnp.expm1(-dt))
    inp = {}
    inp['x'] = nrm((BATCH, SEQ, D), 1.0)
    inp['c'] = nrm((BATCH, D), 1.0)
    inp['ctx'] = nrm((BATCH, CTX_LEN, D), 1.0)
    inp['c_ctx'] = nrm((D,), 1.0)
    inp['l0_mod_w'] = nrm((D, 6 * D), 0.5 * D ** -0.5)
    inp['l0_mod_b'] = nrm((6 * D,), 0.02)
    inp['l0_norm1'] = gain(D)
    inp['l0_norm2'] = gain(D)
    inp['l0_w_in'] = nrm((D, EVEN_IN), D ** -0.5)
    inp['l0_w_out'] = nrm((MIX_WIDTH, D), MIX_WIDTH ** -0.5)
    inp['l0_mlp_w1'] = nrm((D, MLP_HIDDEN), D ** -0.5)
    inp['l0_mlp_w2'] = nrm((MLP_HIDDEN, D), MLP_HIDDEN ** -0.5)
    inp['l0_lam_q1'] = nrm((A_QK_DIM,), 0.1)
    inp['l0_lam_k1'] = nrm((A_QK_DIM,), 0.1)
    inp['l0_lam_q2'] = nrm((A_QK_DIM,), 0.1)
    inp['l0_lam_k2'] = nrm((A_QK_DIM,), 0.1)
    inp['l0_subln'] = gain(A_V_DIM)
    inp['l0_mu'] = unif((B_PROJ,), 0.0, 1.0)
    inp['l0_w0_f'] = unif((B_WIDTH,), -6.0, -1.0)
    inp['l0_w2_f'] = nrm((B_DECAY_LORA, B_WIDTH), 0.1 * B_DECAY_LORA ** -0.5)
    inp['l0_w0_b'] = unif((B_WIDTH,), -6.0, -1.0)
    inp['l0_w2_b'] = nrm((B_DECAY_LORA, B_WIDTH), 0.1 * B_DECAY_LORA ** -0.5)
    inp['l0_a0'] = nrm((B_WIDTH,), 0.1)
    inp['l0_a2'] = nrm((B_AAA_LORA, B_WIDTH), B_AAA_LORA ** -0.5)
    inp['l0_g2'] = nrm((B_GATE_LORA, B_WIDTH), B_GATE_LORA ** -0.5)
    inp['l0_k_k'] = 0.85 + nrm((B_WIDTH,), 0.05)
    inp['l0_k_a'] = gain(B_WIDTH)
    inp['l0_r_k'] = nrm((B_WIDTH,), 0.1)
    inp['l0_lnx_w'] = gain(B_WIDTH)
    inp['l0_lnx_b'] = nrm((B_WIDTH,), 0.02)
    inp['l1_mod_w'] = nrm((D, 6 * D), 0.5 * D ** -0.5)
    inp['l1_mod_b'] = nrm((6 * D,), 0.02)
    inp['l1_norm1'] = gain(D)
    inp['l1_norm2'] = gain(D)
    inp['l1_w_in'] = nrm((D, ODD_IN), D ** -0.5)
    inp['l1_w_out'] = nrm((MIX_WIDTH, D), MIX_WIDTH ** -0.5)
    inp['l1_mlp_w1'] = nrm((D, MLP_HIDDEN), D ** -0.5)
    inp['l1_mlp_w2'] = nrm((MLP_HIDDEN, D), MLP_HIDDEN ** -0.5)
    inp['l1_conv_w'] = nrm((C_CONV, C_CONV_DIM), C_CONV ** -0.5)
    inp['l1_conv_b'] = nrm((C_CONV_DIM,), 0.02)
    inp['l1_dt_bias_f'] = dt_bias(C_HEADS)
    inp['l1_a_log_f'] = jnp.log(unif((C_HEADS,), 1.0, 16.0))
    inp['l1_dt_bias_b'] = dt_bias(C_HEADS)
    inp['l1_a_log_b'] = jnp.log(unif((C_HEADS,), 1.0, 16.0))
    inp['l1_d_skip'] = gain(C_HEADS)
    inp['l1_ssm_norm'] = gain(C_WIDTH)
    inp['l1_gk_up_f'] = nrm((D_GATE_LORA, D_HEADS * D_KEY), D_GATE_LORA ** -0.5)
    inp['l1_gk_b_f'] = nrm((D_HEADS * D_KEY,), 0.1)
    inp['l1_gk_up_b'] = nrm((D_GATE_LORA, D_HEADS * D_KEY), D_GATE_LORA ** -0.5)
    inp['l1_gk_b_b'] = nrm((D_HEADS * D_KEY,), 0.1)
    inp['l1_gla_norm'] = gain(D_VAL)
    inp['final_norm'] = gain(D)
    return inp


def reference(x, c, ctx, c_ctx,
              l0_mod_w, l0_mod_b, l0_norm1, l0_norm2, l0_w_in, l0_w_out, l0_mlp_w1, l0_mlp_w2,
              l0_lam_q1, l0_lam_k1, l0_lam_q2, l0_lam_k2, l0_subln,
              l0_mu, l0_w0_f, l0_w2_f, l0_w0_b, l0_w2_b, l0_a0, l0_a2, l0_g2, l0_k_k, l0_k_a, l0_r_k,
              l0_lnx_w, l0_lnx_b,
              l1_mod_w, l1_mod_b, l1_norm1, l1_norm2, l1_w_in, l1_w_out, l1_mlp_w1, l1_mlp_w2,
              l1_conv_w, l1_conv_b, l1_dt_bias_f, l1_a_log_f, l1_dt_bias_b, l1_a_log_b, l1_d_skip, l1_ssm_norm,
              l1_gk_up_f, l1_gk_b_f, l1_gk_up_b, l1_gk_b_b, l1_gla_norm,
              final_norm):
    shared = [
        (l0_mod_w, l0_mod_b, l0_norm1, l0_norm2, l0_w_in, l0_w_out, l0_mlp_w1, l0_mlp_w2),
        (l1_mod_w, l1_mod_b, l1_norm1, l1_norm2, l1_w_in, l1_w_out, l1_mlp_w1, l1_mlp_w2),
    ]
    for i in range(DEPTH):
        mod_w, mod_b, norm1, norm2, w_in, w_out, mlp_w1, mlp_w2 = shared[i]
        need_ctx = i < DEPTH - 1
        m_l = (jax.nn.silu(c) @ mod_w + mod_b)[:, None, :]
        m_c = jax.nn.silu(c_ctx) @ mod_w + mod_b
        sh1_l, sc1_l, g1_l, sh2_l, sc2_l, g2_l = jnp.split(m_l, 6, axis=-1)
        sh1_c, sc1_c, g1_c, sh2_c, sc2_c, g2_c = jnp.split(m_c, 6, axis=-1)
        p_lat = _ada(x, norm1, sh1_l, sc1_l) @ w_in
        p_ctx = _ada(ctx, norm1, sh1_c, sc1_c) @ w_in
        if i % 2 == 0:
            o_l, o_c = even_mixer(p_lat, p_ctx, l0_lam_q1, l0_lam_k1, l0_lam_q2, l0_lam_k2, l0_subln,
                                  l0_mu, l0_w0_f, l0_w2_f, l0_w0_b, l0_w2_b, l0_a0, l0_a2, l0_g2,
                                  l0_k_k, l0_k_a, l0_r_k, l0_lnx_w, l0_lnx_b, i, need_ctx)
        else:
            o_l, o_c = odd_mixer(p_lat, p_ctx, l1_conv_w, l1_conv_b, l1_dt_bias_f, l1_a_log_f,
                                 l1_dt_bias_b, l1_a_log_b, l1_d_skip, l1_ssm_norm,
                                 l1_gk_up_f, l1_gk_b_f, l1_gk_up_b, l1_gk_b_b, l1_gla_norm, need_ctx)
        x = x + g1_l * (o_l.astype(x.dtype) @ w_out)
        x = x + g2_l * _sq_relu_mlp(_ada(x, norm2, sh2_l, sc2_l), mlp_w1, mlp_w2)
        if need_ctx:
            ctx = ctx + g1_c * (o_c.astype(ctx.dtype) @ w_out)
            ctx = ctx + g2_c * _sq_relu_mlp(_ada(ctx, norm2, sh2_c, sc2_c), mlp_w1, mlp_w2)
    return rmsnorm(x, final_norm)
```

```python
import numpy as np
import concourse.bass as bass
import concourse.mybir as mybir
from concourse.bass_utils import run_bass_kernel_spmd
from contextlib import ExitStack

F32 = mybir.dt.float32
BF16 = mybir.dt.bfloat16
AF = mybir.ActivationFunctionType
ALU = mybir.AluOpType
AX = mybir.AxisListType

SEM_LIMIT = 30000


class Res:
    __slots__ = ("name", "w", "r")

    def __init__(self, name):
        self.name = name
        self.w = None
        self.r = []


class V:
    __slots__ = ("ap", "res")

    def __init__(self, ap, res):
        self.ap = ap
        self.res = res if isinstance(res, (list, tuple)) else [res]


class T:
    def __init__(self, fw, name, shape, dtype, space="sbuf", kind="Internal", stack=None):
        self.name = name
        self.shape = shape
        self.dtype = dtype
        nc = fw.nc
        if space == "sbuf":
            st = stack if stack is not None else fw.stack
            if st is not None:
                self.h = st.enter_context(nc.sbuf_tensor(name, list(shape), dtype))
            else:
                self.h = nc.alloc_sbuf_tensor(name, list(shape), dtype)
        elif space == "psum":
            self.h = nc.alloc_psum_tensor(name, list(shape), dtype)
        else:
            self.h = nc.dram_tensor(name, list(shape), dtype, kind=kind).ap()
        self.res = Res(name)

    def __getitem__(self, key):
        return V(self.h[key], self.res)

    def v(self, ap):
        return V(ap, self.res)


class Eng:
    def __init__(self, fw, name, eng):
        self.fw = fw
        self.name = name
        self.eng = eng
        self.sem = fw.nc.alloc_semaphore(f"s_{name}_0")
        self.nsem = 1
        self.cnt = 0
        self.seen = {}
        self.ninst = 0

    def rotate(self):
        if self.cnt >= SEM_LIMIT:
            self.sem = self.fw.nc.alloc_semaphore(f"s_{self.name}_{self.nsem}")
            self.nsem += 1
            self.cnt = 0


class FW:
    def __init__(self, nc, n_dma_sems=6):
        self.nc = nc
        self.E = {
            "pe": Eng(self, "pe", nc.tensor),
            "dve": Eng(self, "dve", nc.vector),
            "act": Eng(self, "act", nc.scalar),
            "pool": Eng(self, "pool", nc.gpsimd),
            "sp": Eng(self, "sp", nc.sync),
        }
        self.dma_pools = {}
        for en, n in (("sp", n_dma_sems), ("pool", 4)):
            self.dma_pools[en] = [[nc.alloc_semaphore(f"s_dma_{en}_{i}"), 0, f"{en}_{i}", 0] for i in range(n)]
        self.dma_sems = self.dma_pools["sp"] + self.dma_pools["pool"]
        self.dma_rr = {"sp": 0, "pool": 0}
        self.ndma = 0
        self.stack = None
        self.uid = 0

    def barrier(self):
        deps = [(E.sem, E.cnt) for E in self.E.values() if E.cnt > 0]
        deps += [(s[0], s[1]) for s in self.dma_sems if s[1] > 0]
        for E in self.E.values():
            for sem, val in deps:
                if E.seen.get(id(sem), -1) >= val:
                    continue
                E.eng.wait_ge(sem, val)
                E.seen[id(sem)] = val

    def sb(self, name, shape, dtype=None):
        self.uid += 1
        return T(self, f"{name}_{self.uid}", shape, dtype or F32)

    def _waits(self, E, reads, writes, skip_same_pe=True):
        need = {}

        def add(dep):
            if dep is None:
                return
            sem, val = dep
            k = id(sem)
            if k not in need or need[k][1] < val:
                need[k] = (sem, val)

        for v in reads:
            for r in v.res:
                add(r.w)
        for v in writes:
            for r in v.res:
                add(r.w)
                for d in r.r:
                    add(d)
        for k, (sem, val) in need.items():
            if E.name == "pe" and sem is E.sem and skip_same_pe:
                continue
            if E.seen.get(k, -1) >= val:
                continue
            E.eng.wait_ge(sem, val)
            E.seen[k] = val

    def _commit(self, dep, reads, writes):
        for v in writes:
            for r in v.res:
                r.w = dep
                r.r = []
        wset = set(id(r) for v in writes for r in v.res)
        for v in reads:
            for r in v.res:
                if id(r) in wset:
                    continue
                r.r = [d for d in r.r if d[0] is not dep[0]]
                r.r.append(dep)

    def op(self, en, fn, reads, writes):
        E = self.E[en]
        E.rotate()
        self._waits(E, reads, writes)
        inst = fn(E.eng)
        E.cnt += 1
        E.ninst += 1
        inst.then_inc(E.sem, 1)
        self._commit((E.sem, E.cnt), reads, writes)
        return inst

    def dma(self, out, in_, en="sp", **kw):
        E = self.E[en]
        E.rotate()
        pool_ = self.dma_pools[en]
        slot = pool_[self.dma_rr[en]]
        self.dma_rr[en] = (self.dma_rr[en] + 1) % len(pool_)
        if slot[1] >= SEM_LIMIT:
            E.eng.wait_ge(slot[0], slot[1])
            slot[3] += 1
            slot[0] = self.nc.alloc_semaphore(f"s_dma_{slot[2]}_{slot[3]}")
            slot[1] = 0
        sem, cnt = slot[0], slot[1]
        self._waits(E, [in_], [out])
        if cnt > 0 and E.seen.get(id(sem), -1) < cnt:
            E.eng.wait_ge(sem, cnt)
            E.seen[id(sem)] = cnt
        inst = E.eng.dma_start(out=out.ap, in_=in_.ap, **kw)
        inst.then_inc(sem, 16)
        slot[1] = cnt + 16
        self.ndma += 1
        self._commit((sem, cnt + 16), [in_], [out])
        return inst

    def finish(self, outs):
        E = self.E["sp"]
        for t in outs:
            if t.res.w is not None:
                sem, val = t.res.w
                E.eng.wait_ge(sem, val)

    def mm(self, out, lhsT, rhs, start=True, stop=True):
        return self.op("pe", lambda e: e.matmul(out.ap, lhsT.ap, rhs.ap, start=start, stop=stop),
                       [lhsT, rhs], [out])

    def tr(self, out, in_, ident):
        return self.op("pe", lambda e: e.transpose(out.ap, in_.ap, ident.ap), [in_, ident], [out])

    def act(self, out, in_, func, bias=None, scale=None, accum=None, en="act"):
        reads = [in_]
        kw = {}
        if bias is not None:
            if isinstance(bias, V):
                reads.append(bias)
                kw["bias"] = bias.ap
            else:
                kw["bias"] = bias
        if scale is not None:
            if isinstance(scale, V):
                reads.append(scale)
                kw["scale"] = scale.ap
            else:
                kw["scale"] = scale
        writes = [out]
        if accum is not None:
            kw["accum_out"] = accum.ap
            writes.append(accum)
        return self.op(en, lambda e: e.activation(out.ap, in_.ap, func, **kw), reads, writes)

    def tt(self, out, a, b, op, en="dve"):
        return self.op(en, lambda e: e.tensor_tensor(out.ap, a.ap, b.ap, op), [a, b], [out])

    def ts(self, out, a, s1, s2, op0, op1=None, accum=None, en="dve"):
        reads = [a]
        if isinstance(s1, V):
            reads.append(s1)
        if isinstance(s2, V):
            reads.append(s2)
        x1 = s1.ap if isinstance(s1, V) else s1
        x2 = s2.ap if isinstance(s2, V) else s2
        kw = {}
        writes = [out]
        if accum is not None:
            kw["accum_out"] = accum.ap
            writes.append(accum)
        if op1 is None:
            return self.op(en, lambda e: e.tensor_scalar(out.ap, a.ap, x1, None, op0, **kw), reads, writes)
        return self.op(en, lambda e: e.tensor_scalar(out.ap, a.ap, x1, x2, op0, op1, **kw), reads, writes)

    def stt(self, out, a, s, b, op0, op1, en="dve"):
        reads = [a, b]
        if isinstance(s, V):
            reads.append(s)
        x = s.ap if isinstance(s, V) else s
        return self.op(en, lambda e: e.scalar_tensor_tensor(out.ap, a.ap, x, b.ap, op0, op1), reads, [out])

    def cp(self, out, in_, en="dve"):
        if en == "act":
            return self.op(en, lambda e: e.copy(out.ap, in_.ap), [in_], [out])
        return self.op(en, lambda e: e.tensor_copy(out.ap, in_.ap), [in_], [out])

    def red(self, out, in_, op=None, axis=None, en="dve"):
        op = op or ALU.add
        axis = axis or AX.X
        return self.op(en, lambda e: e.tensor_reduce(out.ap, in_.ap, axis, op), [in_], [out])

    def memset(self, out, val, en="dve"):
        return self.op(en, lambda e: e.memset(out.ap, val), [], [out])

    def recip(self, out, in_):
        return self.op("dve", lambda e: e.reciprocal(out.ap, in_.ap), [in_], [out])
NT = 4352
NCTX = 256
NLAT = 4096
D = 1024
EPS = 1e-6
GROUPS512 = [(0, 256, 1)] + [(256 + 512 * i, 512, 0) for i in range(8)]
GROUPS256 = [(0, 256, 1)] + [(256 + 256 * i, 256, 0) for i in range(16)]


class K:
    def __init__(self, debug_outs=()):
        self.nc = bass.Bass("TRN2", target_bir_lowering=False)
        self.fw = FW(self.nc)
        self.debug_outs = set(debug_outs)
        self.dram = {}
        fw = self.fw
        self.ps = [T(fw, f"ps{i}", [128, 512], F32, "psum") for i in range(8)]
        self.ident = T(fw, "ident", [128, 128], F32)
        self.identb = T(fw, "identb", [128, 128], BF16)
        fw.memset(self.ident[:], 0.0)
        fw.op("pool", lambda e: e.affine_select(self.ident.h[:], self.ident.h[:], pattern=[[-1, 128]],
                                                compare_op=ALU.not_equal, fill=1.0, base=0, channel_multiplier=1),
              [self.ident[:]], [self.ident[:]])
        fw.cp(self.identb[:], self.ident[:])
        self.rr = 0

    def din(self, name, shape):
        t = T(self.fw, name, list(shape), F32, "dram", "ExternalInput")
        self.dram[name] = t
        return t

    def dscratch(self, name, shape, dtype=None):
        kind = "ExternalOutput" if name in self.debug_outs else "Internal"
        t = T(self.fw, name, list(shape), dtype or F32, "dram", kind)
        self.dram[name] = t
        return t

    def bank(self, lo=4, n=4):
        self.rr += 1
        return self.ps[lo + self.rr % n]

    def stage(self):
        k = self

        class _S:
            def __enter__(s):
                k.fw.barrier()
                s.es = ExitStack()
                s.es.__enter__()
                k.fw.stack = s.es
                return s

            def __exit__(s, *a):
                k.fw.barrier()
                k.fw.stack = None
                s.es.__exit__(*a)
                return False
        return _S()

    def load_cols(self, dst, vec_t, n, off=0):
        fw = self.fw
        tmp = fw.sb("lc_tmp", [128, 128])
        hh = vec_t.h if len(vec_t.shape) == 1 else vec_t.h.rearrange("a b -> (a b)")
        src = hh[off:off + n * 128].rearrange("(n p) -> n p", p=128)
        fw.dma(tmp[0:n, :], V(src, vec_t.res))
        b = self.bank(0, 4)
        fw.tr(b[:, 0:n], tmp[0:n, :], self.ident[0:n, 0:n])
        fw.cp(dst, b[:, 0:n])

    def load_wbf16(self, Wd, N, name, krows=1024):
        fw = self.fw
        kc = krows // 128
        Wsb = fw.sb(name, [128, kc, N], BF16)
        src = Wd.h.rearrange("(k p) n -> p k n", p=128)
        step = 512
        for c0 in range(0, N, step):
            c1 = min(N, c0 + step)
            fw.dma(Wsb[:, :, c0:c1], V(src[:, :, c0:c1], Wd.res), en="pool")
        return Wsb

    def stage_mod(self, L, cvec, mod_w, mod_b, norm1, norm2, lstack=None):
        fw = self.fw
        modA = T(fw, f"modA{L}", [128, 2, 2, 8], F32, stack=lstack)
        modB = T(fw, f"modB{L}", [128, 2, 2, 8], F32, stack=lstack)
        Gbc = T(fw, f"Gbc{L}", [128, 2, 2, 1024], F32, stack=lstack)
        with self.stage():
            cT = fw.sb("cT", [128, 16])
            self.load_cols(cT[:, 0:16], cvec, 16)
            sc = fw.sb("sc", [128, 16])
            fw.act(sc[:], cT[:], AF.Silu)
            scb = fw.sb("scb", [128, 2, 8, 128])
            for w in range(2):
                for kk in range(8):
                    fw.cp(scb[:, w, kk, :], V(sc.h[:, kk * 2 + w:kk * 2 + w + 1].to_broadcast([128, 128]), sc.res))
            modbT = fw.sb("modbT", [128, 48])
            self.load_cols(modbT[:, 0:48], mod_b, 48)
            nrm = fw.sb("nrm", [128, 2, 8])
            self.load_cols(nrm[:, 0, :], norm1, 8)
            self.load_cols(nrm[:, 1, :], norm2, 8)
            modT = fw.sb("modT", [128, 2, 6, 8])
            mbb = fw.sb("mbb", [128, 2, 1024])
            fw.dma(mbb[:, 0, :], V(mod_b.h[2048:3072].partition_broadcast(128), mod_b.res))
            fw.dma(mbb[:, 1, :], V(mod_b.h[5120:6144].partition_broadcast(128), mod_b.res))
            wsrc = mod_w.h.rearrange("(k p) n -> p k n", p=128)
            wb = [fw.sb("modw", [128, 8, 1024]) for _ in range(2)]
            for j in range(6):
                wblk = wb[j % 2]
                fw.dma(wblk[:], V(wsrc[:, :, j * 1024:(j + 1) * 1024], mod_w.res))
                for c in range(8):
                    b = self.bank(0, 4)
                    for kk in range(8):
                        fw.mm(b[:, 0:2], wblk[:, kk, c * 128:(c + 1) * 128], sc[:, kk * 2:kk * 2 + 2],
                              start=(kk == 0), stop=(kk == 7))
                    fw.ts(modT[:, :, j, c], b[:, 0:2], modbT[:, j * 8 + c:j * 8 + c + 1], None, ALU.add)
                if j in (2, 5):
                    sub = 0 if j == 2 else 1
                    for w in range(2):
                        for half in range(2):
                            b = self.bank(4, 4)
                            for kk in range(8):
                                fw.mm(b[:, :], scb[:, w, kk, :], wblk[:, kk, half * 512:(half + 1) * 512],
                                      start=(kk == 0), stop=(kk == 7))
                            fw.tt(Gbc[:, w, sub, half * 512:(half + 1) * 512], b[:, :],
                                  mbb[:, sub, half * 512:(half + 1) * 512], ALU.add)
            for w in range(2):
                for sub in range(2):
                    jsh, jsc = (0, 1) if sub == 0 else (3, 4)
                    fw.ts(modA[:, w, sub, :], modT[:, w, jsc, :], 1.0, None, ALU.add)
                    fw.tt(modA[:, w, sub, :], modA[:, w, sub, :], nrm[:, sub, :], ALU.mult)
                    fw.cp(modB[:, w, sub, :], modT[:, w, jsh, :])
        return modA, modB, Gbc

    def norm_tile(self, xt, xn_dst_fn, which, sub, modA, modB, tix, eps=EPS):
        fw = self.fw
        junk = self.n_junk[tix % 2]
        ssq = self.n_ssq[tix % 2]
        xs = self.n_xs[tix % 2]
        fw.memset(ssq[:], 0.0)
        fw.act(junk[:], xt, AF.Square, accum=ssq[:])
        fw.ts(ssq[:], ssq[:], 1.0 / 1024, eps, ALU.mult, ALU.add)
        fw.act(ssq[:], ssq[:], AF.Sqrt)
        fw.recip(ssq[:], ssq[:])
        fw.ts(xs[:], xt, ssq[:, 0:1], None, ALU.mult)
        pb = (self.ps[0], self.ps[1]) if tix % 2 == 0 else (self.ps[2], self.ps[3])
        for k in range(8):
            fw.tr(pb[k // 4][:, (k % 4) * 128:(k % 4 + 1) * 128], xs[:, k * 128:(k + 1) * 128], self.ident[:])
        for k in range(8):
            fw.act(xn_dst_fn(k), pb[k // 4][:, (k % 4) * 128:(k % 4 + 1) * 128], AF.Identity,
                   scale=modA[:, which, sub, k:k + 1], bias=modB[:, which, sub, k:k + 1])

    def alloc_norm_bufs(self):
        fw = self.fw
        self.n_junk = [fw.sb("n_junk", [128, 1024])] * 2
        self.n_ssq = [fw.sb("n_ssq", [128, 1]) for _ in range(2)]
        self.n_xs = [fw.sb("n_xs", [128, 1024])] * 2

    def stage_inproj(self, xsrc, Wd, N, modA, modB, fm_specs, tm_specs, fm_out, tm_out, groups=GROUPS512):
        fw = self.fw
        with self.stage():
            Wsb = self.load_wbf16(Wd, N, "Win")
            self.alloc_norm_bufs()
            xts = [fw.sb("xt", [128, 1024]) for _ in range(2)]
            xns = [fw.sb("xn", [128, 8, 512], BF16) for _ in range(2)]
            stg = [fw.sb("stg", [128, 512]) for _ in range(4)]
            tix = 0
            ev = 0
            for gi, (t0, ntok, which) in enumerate(groups):
                xn = xns[gi % 2]
                for ti in range(ntok // 128):
                    xt = xts[tix % 2]
                    fw.dma(xt[:], xsrc[t0 + ti * 128:t0 + (ti + 1) * 128, :])
                    self.norm_tile(xt[:], lambda k: xn[:, k, ti * 128:(ti + 1) * 128], which, 0, modA, modB, tix)
                    tix += 1
                for (c0, w, r0) in fm_specs:
                    b = self.bank(4, 4)
                    for k in range(8):
                        fw.mm(b[0:w, 0:ntok], Wsb[:, k, c0:c0 + w], xn[:, k, 0:ntok], start=(k == 0), stop=(k == 7))
                    s = stg[ev % 4]
                    fw.cp(s[0:w, 0:ntok], b[0:w, 0:ntok], en=("dve" if ev % 2 == 0 else "act"))
                    ev += 1
                    fw.dma(fm_out[r0:r0 + w, t0:t0 + ntok], s[0:w, 0:ntok])
                for ti in range(ntok // 128):
                    for (c0, w, d0) in tm_specs:
                        b = self.bank(4, 4)
                        for k in range(8):
                            fw.mm(b[:, 0:w], xn[:, k, ti * 128:(ti + 1) * 128], Wsb[:, k, c0:c0 + w],
                                  start=(k == 0), stop=(k == 7))
                        s = stg[ev % 4]
                        fw.cp(s[:, 0:w], b[:, 0:w], en=("dve" if ev % 2 == 0 else "act"))
                        ev += 1
                        fw.dma(tm_out[t0 + ti * 128:t0 + (ti + 1) * 128, d0:d0 + w], s[:, 0:w])

    def stage_outproj(self, o_fm, xsrc, xdst, Wd, Gbc, tok_tiles):
        fw = self.fw
        with self.stage():
            Wsb = self.load_wbf16(Wd, 1024, "Wout")
            osrc = o_fm.h.rearrange("(k p) t -> p k t", p=128)
            oTs = [fw.sb("oT", [128, 8, 128]) for _ in range(2)]
            obs = [fw.sb("ob", [128, 8, 128], BF16) for _ in range(2)]
            xts = [fw.sb("xt", [128, 1024]) for _ in range(2)]
            tmps = [fw.sb("tmp", [128, 1024]) for _ in range(2)]
            xos = [fw.sb("xo", [128, 1024]) for _ in range(2)]
            for i, tt in enumerate(tok_tiles):
                which = 1 if tt < 2 else 0
                tok = slice(tt * 128, (tt + 1) * 128)
                oT, ob, xt, tmp, xo = oTs[i % 2], obs[i % 2], xts[i % 2], tmps[i % 2], xos[i % 2]
                fw.dma(oT[:], V(osrc[:, :, tok], o_fm.res))
                fw.cp(ob[:], oT[:], en="pool")
                fw.dma(xt[:], xsrc[tok, :])
                for half in range(2):
                    hs = slice(half * 512, (half + 1) * 512)
                    b = self.bank(4, 4)
                    for k in range(8):
                        fw.mm(b[:, :], ob[:, k, :], Wsb[:, k, hs], start=(k == 0), stop=(k == 7))
                    fw.tt(tmp[:, hs], b[:, :], Gbc[:, which, 0, hs], ALU.mult)
                    fw.tt(xo[:, hs], tmp[:, hs], xt[:, hs], ALU.add, en="pool")
                fw.dma(xdst[tok, :], xo[:])

    def stage_mlp(self, xsrc, xdst, W1d, W2d, modA, modB, Gbc, groups, final_norm=None, out_t=None):
        fw = self.fw
        with self.stage():
            W1 = self.load_wbf16(W1d, 4096, "W1")
            W2 = self.load_wbf16(W2d, 1024, "W2", krows=4096)
            self.alloc_norm_bufs()
            xts = [fw.sb("xt", [128, 1024]) for _ in range(2)]
            xr = fw.sb("xr", [128, 1024])
            xns = [fw.sb("xn", [128, 8, 256], BF16) for _ in range(2)]
            h1s = [fw.sb("h1", [128, 32, 256], BF16) for _ in range(1)]
            rs = [fw.sb("r", [128, 256]) for _ in range(2)]
            tmps = [fw.sb("tmp", [128, 512]) for _ in range(2)]
            xos = [fw.sb("xo", [128, 1024])] * 2
            if final_norm is not None:
                fnb = fw.sb("fnb", [128, 1024])
                fw.dma(fnb[:], V(final_norm.h[:].partition_broadcast(128), final_norm.res))
                fssq = [fw.sb("fssq", [128, 1]) for _ in range(2)]
            tix = 0
            for gi, (t0, ntok, which) in enumerate(groups):
                xn = xns[gi % 2]
                h1 = h1s[0]
                nt = ntok // 128
                gx = []
                for ti in range(nt):
                    xt = xts[tix % 2]
                    fw.dma(xt[:], xsrc[t0 + ti * 128:t0 + (ti + 1) * 128, :])
                    self.norm_tile(xt[:], lambda k: xn[:, k, ti * 128:(ti + 1) * 128], which, 1, modA, modB, tix)
                    tix += 1
                for c in range(32):
                    b = self.bank(4, 4)
                    for k in range(8):
                        fw.mm(b[:, 0:ntok], W1[:, k, c * 128:(c + 1) * 128], xn[:, k, 0:ntok],
                              start=(k == 0), stop=(k == 7))
                    r = rs[c % 2]
                    fw.act(r[:, 0:ntok], b[:, 0:ntok], AF.Relu)
                    fw.tt(h1[:, c, 0:ntok], r[:, 0:ntok], r[:, 0:ntok], ALU.mult, en=("pool" if c % 2 else "dve"))
                for ti in range(nt):
                    xo = xos[ti % 2]
                    xt = xr
                    fw.dma(xr[:], xsrc[t0 + ti * 128:t0 + (ti + 1) * 128, :])
                    for half in range(2):
                        hs = slice(half * 512, (half + 1) * 512)
                        b = self.bank(4, 4)
                        for c in range(32):
                            fw.mm(b[:, :], h1[:, c, ti * 128:(ti + 1) * 128], W2[:, c, hs],
                                  start=(c == 0), stop=(c == 31))
                        tmp = tmps[half]
                        fw.tt(tmp[:], b[:, :], Gbc[:, which, 1, hs], ALU.mult)
                        fw.tt(xo[:, hs], tmp[:], xt[:, hs], ALU.add, en="pool")
                    tok = slice(t0 + ti * 128, t0 + (ti + 1) * 128)
                    if final_norm is None:
                        fw.dma(xdst[tok, :], xo[:])
                    else:
                        ss = fssq[ti % 2]
                        junk = self.n_junk[ti % 2]
                        fw.memset(ss[:], 0.0)
                        fw.act(junk[:], xo[:], AF.Square, accum=ss[:])
                        fw.ts(ss[:], ss[:], 1.0 / 1024, EPS, ALU.mult, ALU.add)
                        fw.act(ss[:], ss[:], AF.Sqrt)
                        fw.recip(ss[:], ss[:])
                        fw.stt(junk[:], xo[:], ss[:, 0:1], fnb[:], ALU.mult, ALU.mult)
                        fw.dma(out_t[t0 - NCTX + ti * 128:t0 - NCTX + (ti + 1) * 128, :], junk[:])
def stage_attention(k, I, fm, tm, ofm, need_ctx=True, lam_init=0.2):
    fw = k.fw
    with k.stage():
        cosT = fw.sb("cosT", [64, NLAT])
        sinT = fw.sb("sinT", [64, NLAT])
        pT = fw.sb("pT", [64, 64])
        fw.dma(cosT[:], I['rope_cos'][:, :])
        fw.dma(sinT[:], I['rope_sin'][:, :])
        fw.dma(pT[:], I['rope_pT'][:, :])
        ones_c = fw.sb("ones_c", [128, 1])
        ones_r = fw.sb("ones_r", [1, 128])
        fw.memset(ones_c[:], 1.0)
        fw.memset(ones_r[:], 1.0)
        lq = fw.sb("lq", [1, 4, 64])
        for i, n in enumerate(['l0_lam_q1', 'l0_lam_k1', 'l0_lam_q2', 'l0_lam_k2']):
            fw.dma(lq[0:1, i, :], V(I[n].h[0:64].rearrange("(o n) -> o n", o=1), I[n].res))
        prod = fw.sb("prod", [1, 2, 64])
        fw.tt(prod[0:1, 0, :], lq[0:1, 0, :], lq[0:1, 1, :], ALU.mult)
        fw.tt(prod[0:1, 1, :], lq[0:1, 2, :], lq[0:1, 3, :], ALU.mult)
        s2 = fw.sb("s2", [1, 2])
        fw.red(s2[:], prod[:], ALU.add)
        fw.act(s2[:], s2[:], AF.Exp)
        nl = fw.sb("nl", [1, 1])
        fw.tt(nl[:], s2[0:1, 1:2], s2[0:1, 0:1], ALU.subtract)
        fw.ts(nl[:], nl[:], -lam_init, None, ALU.add)
        b = k.bank(0, 4)
        fw.mm(b[:, 0:1], ones_r[0:1, :], nl[0:1, 0:1])
        nlam = fw.sb("nlam", [128, 1])
        fw.cp(nlam[:], b[:, 0:1])
        sublnc = fw.sb("sublnc", [128, 1])
        k.load_cols(sublnc[:, 0:1], I['l0_subln'], 1)
        fw.ts(sublnc[:], sublnc[:], 1.0 - lam_init, None, ALU.mult)
        stag = [fw.sb("stag", [128, NT]) for _ in range(2)]
        Qb = [fw.sb("Qb", [64, NT], BF16) for _ in range(2)]
        Kb = [fw.sb("Kb", [64, NT], BF16) for _ in range(2)]
        vext = fw.sb("vext", [128, 34, 129], BF16)
        fw.memset(vext[:, :, 128:129], 1.0)
        PTs = [fw.sb("PT", [128, 34, 256], BF16) for _ in range(2)]
        Om = [fw.sb("Om", [128, 4, 129]) for _ in range(2)]
        stg = [fw.sb("astg", [128, 512]) for _ in range(2)]
        tmp = [fw.sb("atmp", [64, 512]) for _ in range(2)]
        sqc = [fw.sb("sqc", [64, 512]) for _ in range(2)]
        nmax = fw.sb("nmax", [1, 4, 16])
        nm1 = fw.sb("nm1", [1, 4])
        negM = fw.sb("negM", [128, 2])
        o_t = [fw.sb("o_t", [128, 128]) for _ in range(2)]
        junk = [fw.sb("ajunk", [128, 128]) for _ in range(2)]
        r12 = [fw.sb("r12", [128, 2]) for _ in range(2)]
        ss = [fw.sb("ass", [128, 1]) for _ in range(2)]
        vsrc = tm.h.rearrange("(n p) c -> p n c", p=128)
        chunks = [(0, 256)] + [(256 + c * 512, 512) for c in range(8)]
        si = 0
        ptc = 0
        stc = 0
        for h in range(4):
            fw.memset(nmax[:], 0.0)
            for m in range(2):
                for which, (row0, dst) in enumerate([(h * 128 + m * 64, Qb[m]), (512 + h * 128 + m * 64, Kb[m])]):
                    st = stag[si % 2]
                    si += 1
                    fw.dma(st[0:64, :], fm[row0:row0 + 64, :])
                    slot = m * 2 + which
                    for ci, (c0, n) in enumerate(chunks):
                        sq = sqc[ci % 2]
                        fw.tt(sq[:, 0:n], st[0:64, c0:c0 + n], st[0:64, c0:c0 + n], ALU.mult, en="pool")
                        bb = k.bank(0, 4)
                        fw.mm(bb[0:1, 0:n], ones_c[0:64, 0:1], sq[:, 0:n])
                        fw.red(nmax[0:1, slot, ci:ci + 1], bb[0:1, 0:n], ALU.max)
                    fw.cp(dst[:, 0:256], st[0:64, 0:256], en="pool")
                    for c in range(8):
                        cs = slice(256 + c * 512, 256 + (c + 1) * 512)
                        ts_ = slice(c * 512, (c + 1) * 512)
                        bb = k.bank(0, 4)
                        fw.mm(bb[0:64, :], pT[:, :], st[0:64, cs])
                        t = tmp[c % 2]
                        fw.tt(t[:], bb[0:64, :], sinT[:, ts_], ALU.mult)
                        fw.tt(st[0:64, cs], st[0:64, cs], cosT[:, ts_], ALU.mult, en="pool")
                        fw.tt(dst[:, cs], st[0:64, cs], t[:], ALU.add)
            fw.red(nm1[:], nmax[:], ALU.max)
            for m in range(2):
                fw.tt(nm1[0:1, m * 2:m * 2 + 1], nm1[0:1, m * 2:m * 2 + 1], nm1[0:1, m * 2 + 1:m * 2 + 2], ALU.mult)
                fw.act(nm1[0:1, m * 2:m * 2 + 1], nm1[0:1, m * 2:m * 2 + 1], AF.Sqrt)
                fw.ts(nm1[0:1, m * 2:m * 2 + 1], nm1[0:1, m * 2:m * 2 + 1], -0.125, None, ALU.mult)
                bb = k.bank(0, 4)
                fw.mm(bb[:, 0:1], ones_r[0:1, :], nm1[0:1, m * 2:m * 2 + 1])
                fw.cp(negM[:, m:m + 1], bb[:, 0:1])
            st = stag[si % 2]
            si += 1
            stv = V(st.h[:, :].rearrange("p (n c) -> p n c", c=128), st.res)
            fw.dma(stv, V(vsrc[:, :, h * 128:(h + 1) * 128], tm.res))
            fw.cp(vext[:, :, 0:128], stv, en="pool")
            groups = [(256 + g * 256, 256, list(range(34))) for g in range(16)]
            if need_ctx:
                groups = [(0, 256, [0, 1])] + groups
            for (q0, nq, kts) in groups:
                for m in range(2):
                    PT = PTs[ptc % 2]
                    ptc += 1
                    for kt in kts:
                        bb = k.bank(0, 4)
                        fw.mm(bb[:, 0:nq], Kb[m][:, kt * 128:(kt + 1) * 128], Qb[m][:, q0:q0 + nq])
                        fw.act(PT[:, kt, 0:nq], bb[:, 0:nq], AF.Exp, scale=0.125, bias=negM[:, m:m + 1])
                    for sub in range(nq // 128):
                        ob = k.bank(4, 4)
                        for i, kt in enumerate(kts):
                            fw.mm(ob[:, 0:129], PT[:, kt, sub * 128:(sub + 1) * 128], vext[:, kt, :],
                                  start=(i == 0), stop=(i == len(kts) - 1))
                        fw.cp(Om[m][:, sub, :], ob[:, 0:129])
                sg = stg[stc % 2]
                stc += 1
                for sub in range(nq // 128):
                    o = o_t[sub % 2]
                    r = r12[sub % 2]
                    s_ = ss[sub % 2]
                    jk = junk[sub % 2]
                    fw.recip(r[:, 0:1], Om[0][:, sub, 128:129])
                    fw.recip(r[:, 1:2], Om[1][:, sub, 128:129])
                    fw.tt(r[:, 1:2], r[:, 1:2], nlam[:], ALU.mult)
                    fw.ts(o[:], Om[0][:, sub, 0:128], r[:, 0:1], None, ALU.mult)
                    fw.stt(o[:], Om[1][:, sub, 0:128], r[:, 1:2], o[:], ALU.mult, ALU.add)
                    fw.tt(jk[:], o[:], o[:], ALU.mult, en="pool")
                    fw.red(s_[:], jk[:], ALU.add)
                    fw.ts(s_[:], s_[:], 1.0 / 128, 1e-5, ALU.mult, ALU.add)
                    fw.act(s_[:], s_[:], AF.Ln)
                    fw.act(s_[:], s_[:], AF.Exp, scale=-0.5)
                    fw.ts(o[:], o[:], s_[:, 0:1], None, ALU.mult)
                    bb = k.bank(0, 4)
                    fw.tr(bb[:, 0:128], o[:], k.ident[:])
                    fw.act(sg[:, sub * 128:(sub + 1) * 128], bb[:, 0:128], AF.Identity, scale=sublnc[:, 0:1])
                fw.dma(ofm[h * 128:(h + 1) * 128, q0:q0 + nq], sg[:, 0:nq])
def stage_gla(k, I, fm, tm, ofm):
    fw = k.fw
    S = k.scr
    QT = [S(f"gl_QT{d}", [256, NT]) for d in range(2)]
    KT = [S(f"gl_KT{d}", [256, NT]) for d in range(2)]
    KTM = [S(f"gl_KTM{d}", [NT, 256]) for d in range(2)]
    CWL = [S(f"gl_CWL{d}", [256, 34]) for d in range(2)]
    YD = [S(f"gl_Y{d}", [512, NT]) for d in range(2)]
    with k.stage():
        masks = fw.sb("masks", [128, 4, 128])
        fw.dma(masks[:], V(I['rk_masks'].h.rearrange("m p t -> p m t"), I['rk_masks'].res))
        up = [fw.sb("upf", [32, 256]), fw.sb("upb", [32, 256])]
        fw.memset(up[0][:], 0.0)
        fw.memset(up[1][:], 0.0)
        fw.dma(up[0][0:16, :], I['l1_gk_up_f'][:, :])
        fw.dma(up[1][0:16, :], I['l1_gk_up_b'][:, :])
        gkb = fw.sb("gkb", [128, 2, 2])
        k.load_cols(gkb[:, 0, :], I['l1_gk_b_f'], 2)
        k.load_cols(gkb[:, 1, :], I['l1_gk_b_b'], 2)
        gd = [fw.sb("gd", [32, 512]) for _ in range(2)]
        fw.memset(gd[0][:], 0.0)
        fw.memset(gd[1][:], 0.0)
        qk = [fw.sb("qk", [128, 512]) for _ in range(4)]
        lg = fw.sb("lg", [128, 512])
        cw = fw.sb("cw", [128, 512])
        icw = fw.sb("icw", [128, 512])
        outs = [fw.sb("outs", [128, 512]) for _ in range(4)]
        tmst = [fw.sb("tmst", [128, 4, 128]) for _ in range(2)]
        lwT = [fw.sb("lwT", [128, 128]) for _ in range(2)]
        cwl = [fw.sb("cwl", [128, 4]) for _ in range(2)]
        oc = 0
        for bi_, (t0, nb, which) in enumerate(GROUPS512):
            nt = nb // 128
            n_ = slice(0, nb)
            g_ = gd[bi_ % 2]
            fw.dma(g_[0:16, n_], fm[2048:2064, t0:t0 + nb])
            for hp in range(2):
                cs = slice(hp * 128, (hp + 1) * 128)
                q_ = qk[(2 * hp) % 4]
                k_ = qk[(2 * hp + 1) % 4]
                fw.dma(q_[:, n_], fm[1024 + hp * 128:1024 + (hp + 1) * 128, t0:t0 + nb])
                fw.dma(k_[:, n_], fm[1280 + hp * 128:1280 + (hp + 1) * 128, t0:t0 + nb])
                for d in range(2):
                    b = k.bank(0, 6)
                    fw.mm(b[:, n_], up[d][0:32, cs], g_[0:32, n_])
                    fw.act(lg[:, n_], b[:, n_], AF.Sigmoid, bias=gkb[:, d, hp:hp + 1])
                    fw.act(lg[:, n_], lg[:, n_], AF.Ln)
                    bi = k.ps[6]
                    for ti in range(nt):
                        b = k.bank(0, 6)
                        fw.tr(b[:, 0:128], lg[:, ti * 128:(ti + 1) * 128], k.ident[:])
                        lt = lwT[ti % 2]
                        fw.cp(lt[:], b[:, 0:128], en=("act" if ti % 2 else "dve"))
                        fw.mm(bi[:, ti * 128:(ti + 1) * 128], lt[:, :], masks[:, 2 * d, :])
                    fw.act(cw[:, n_], bi[:, n_], AF.Exp, scale=1.0 / 16)
                    fw.act(icw[:, n_], bi[:, n_], AF.Exp, scale=-1.0 / 16)
                    o = outs[oc % 4]; oc += 1
                    fw.stt(o[:, n_], q_[:, n_], 0.125, cw[:, n_], ALU.mult, ALU.mult)
                    fw.dma(QT[d][cs, t0:t0 + nb], o[:, n_])
                    o = outs[oc % 4]; oc += 1
                    fw.tt(o[:, n_], k_[:, n_], icw[:, n_], ALU.mult, en="pool")
                    fw.dma(KT[d][cs, t0:t0 + nb], o[:, n_])
                    st = tmst[oc % 2]
                    for ti in range(nt):
                        b = k.bank(0, 6)
                        fw.tr(b[:, 0:128], o[:, ti * 128:(ti + 1) * 128], k.ident[:])
                        fw.cp(st[:, ti, :], b[:, 0:128], en=("act" if ti % 2 else "dve"))
                    dv = KTM[d].h.rearrange("(n p) c -> p n c", p=128)
                    fw.dma(V(dv[:, t0 // 128:t0 // 128 + nt, cs], KTM[d].res), st[:, 0:nt, :])
                    cl = cwl[d]
                    off = 127 if d == 0 else 0
                    fw.cp(cl[:, 0:nt], V(cw.h[:, off:nb:128], cw.res))
                    fw.dma(CWL[d][cs, t0 // 128:t0 // 128 + nt], cl[:, 0:nt])
    ZFM = S("gl_ZFM", [256, NT])
    ZTM = S("gl_ZTM", [NT, 256])
    VTMg = S("gl_VTM", [NT, 512])
    MATS = [S(f"gl_MATS{d}", [2, 34, 128, 8 * 128]) for d in range(2)]
    with k.stage():
        z = fw.sb("zz", [128, 4352])
        fw.memset(z[:], 0.0)
        for r in range(2):
            fw.dma(ZFM[r * 128:(r + 1) * 128, :], z[:, :])
        zv = ZTM.h.rearrange("(n p) c -> p n c", p=128)
        fw.dma(V(zv[:, 0:17, :], ZTM.res), V(z.h[:, :].rearrange("p (n c) -> p n c", c=256), z.res))
        fw.dma(V(zv[:, 17:34, :], ZTM.res), V(z.h[:, :].rearrange("p (n c) -> p n c", c=256), z.res))
        vt = [fw.sb("vt", [128, 512]) for _ in range(2)]
        for tix in range(34):
            tk = slice(tix * 128, (tix + 1) * 128)
            v_ = vt[tix % 2]
            fw.dma(v_[:], tm[tk, 528:1040])
            sv = v_.h[:, :].rearrange("p (a b c e) -> p a b c e", a=2, b=2, c=2)
            dvv = VTMg.h[tk, :].rearrange("p (a c b e) -> p a c b e", a=2, c=2, b=2)
            for hp in range(2):
                for vh in range(2):
                    fw.dma(V(dvv[:, hp, vh, :, :], VTMg.res), V(sv[:, hp, :, vh, :], v_.res))
    _r = [slice(hp * 128, (hp + 1) * 128) for hp in range(2)]
    _j = [slice(j * 128, (j + 1) * 128) for j in range(4)]
    gt = [(_r[hp], hp) for hp in range(2)]
    st = [(_r[j // 2], _r[j // 2], _j[j], _j[j], j // 2) for j in range(4)]
    dplr_gpass_scan(k, I, gt, st, QT, [ZFM, ZFM], KT, [ZFM, ZFM], KTM, [ZTM, ZTM], VTMg, CWL, MATS, YD)
    with k.stage():
        gnc = fw.sb("gnc", [128, 1])
        k.load_cols(gnc[:, 0:1], I['l1_gla_norm'], 1)
        Oavg = fw.sb("Oavg", [128, 128])
        fw.memset(Oavg[:], 1.0 / 128)
        y0 = [fw.sb("y0", [128, 512]) for _ in range(2)]
        y1 = [fw.sb("y1", [128, 512]) for _ in range(2)]
        gg = [fw.sb("gg", [128, 512]) for _ in range(2)]
        sq2 = [fw.sb("sq2", [128, 512]) for _ in range(2)]
        i = 0
        for (t0, nb, which) in GROUPS512[1:]:
            for h in range(4):
                cs = slice(h * 128, (h + 1) * 128)
                n_ = slice(0, nb)
                a, b_, g_, s_ = y0[i % 2], y1[i % 2], gg[i % 2], sq2[i % 2]
                i += 1
                hp_, hh_ = h // 2, h % 2
                for vh in range(2):
                    r0 = (hp_ * 2 + vh) * 128 + hh_ * 64
                    fw.dma(a[vh * 64:(vh + 1) * 64, n_], YD[0][r0:r0 + 64, t0:t0 + nb])
                    fw.dma(b_[vh * 64:(vh + 1) * 64, n_], YD[1][r0:r0 + 64, t0:t0 + nb])
                fw.dma(g_[:, n_], fm[1536 + h * 128:1536 + (h + 1) * 128, t0:t0 + nb])
                fw.tt(a[:, n_], a[:, n_], b_[:, n_], ALU.add, en="pool")
                fw.tt(s_[:, n_], a[:, n_], a[:, n_], ALU.mult, en="pool")
                bv = k.bank(0, 8)
                fw.mm(bv[:, n_], Oavg[:, :], s_[:, n_])
                fw.ts(s_[:, n_], bv[:, n_], EPS, None, ALU.add)
                fw.act(s_[:, n_], s_[:, n_], AF.Sqrt)
                fw.recip(s_[:, n_], s_[:, n_])
                fw.tt(a[:, n_], a[:, n_], s_[:, n_], ALU.mult)
                fw.act(g_[:, n_], g_[:, n_], AF.Silu)
                fw.stt(a[:, n_], a[:, n_], gnc[:, 0:1], g_[:, n_], ALU.mult, ALU.mult)
                fw.dma(ofm[512 + h * 128:512 + (h + 1) * 128, t0:t0 + nb], a[:, n_])
import math as _math


def dplr_gpass_scan(k, I, gtiles, stiles, RT, AT, KT, BT, KTM, BTM, VTM, CWL, MATS, YD):
    fw = k.fw
    with k.stage():
        masks = fw.sb("masks", [128, 4, 128])
        fw.dma(masks[:], V(I['rk_masks'].h.rearrange("m p t -> p m t"), I['rk_masks'].res))
        nmask = fw.sb("nmask", [128, 2, 128])
        fw.ts(nmask[:, 0, :], masks[:, 1, :], -1.0, None, ALU.mult)
        fw.ts(nmask[:, 1, :], masks[:, 3, :], -1.0, None, ALU.mult)
        At = fw.sb("At", [128, NT]); Bt = fw.sb("Bt", [128, NT]); Kt = fw.sb("Kt", [128, NT]); Rt = fw.sb("Rt", [128, NT])
        Nb = [fw.sb("Nb", [128, 2, 128]) for _ in range(2)]
        Pb = [fw.sb("Pb", [128, 2, 128]) for _ in range(2)]
        Wt = [fw.sb("Wt", [128, 2, 128]) for _ in range(2)]
        mst = [fw.sb("mst", [128, 2, 4, 128]) for _ in range(2)]
        mc = 0
        for d in range(2):
            for (cs, hp) in gtiles:
                fw.dma(At[:], AT[d][cs, :]); fw.dma(Bt[:], BT[d][cs, :]); fw.dma(Kt[:], KT[d][cs, :]); fw.dma(Rt[:], RT[d][cs, :])
                for tix in range(34):
                    tk = slice(tix * 128, (tix + 1) * 128)
                    ms = mst[mc % 2]; mc += 1
                    bN, bP = k.bank(0, 8), k.bank(0, 8)
                    b2, b3 = k.bank(0, 8), k.bank(0, 8)
                    for hh in range(2):
                        hs = slice(64 * hh, 64 * hh + 64)
                        c_ = slice(hh * 128, (hh + 1) * 128)
                        fw.mm(bN[:, c_], Bt[hs, tk], At[hs, tk])
                        fw.mm(bP[:, c_], At[hs, tk], Bt[hs, tk])
                        fw.mm(b2[:, c_], Kt[hs, tk], At[hs, tk])
                        fw.mm(b3[:, c_], Bt[hs, tk], Rt[hs, tk])
                    b4 = k.bank(0, 8)
                    for hh in range(2):
                        hs = slice(64 * hh, 64 * hh + 64)
                        c_ = slice(hh * 128, (hh + 1) * 128)
                        fw.mm(b4[:, c_], Kt[hs, tk], Rt[hs, tk])
                    N, P, W = Nb[0], Pb[0], Wt[0]
                    for hh in range(2):
                        c_ = slice(hh * 128, (hh + 1) * 128)
                        fw.tt(N[:, hh, :], bN[:, c_], nmask[:, d, :], ALU.mult)
                        fw.tt(P[:, hh, :], bP[:, c_], nmask[:, 1 - d, :], ALU.mult)
                        fw.tt(ms[:, hh, 1, :], b2[:, c_], masks[:, 2 * d + 1, :], ALU.mult)
                        fw.tt(ms[:, hh, 2, :], b3[:, c_], masks[:, 2 * d, :], ALU.mult)
                        fw.tt(ms[:, hh, 3, :], b4[:, c_], masks[:, 2 * d, :], ALU.mult)
                        fw.tt(W[:, hh, :], N[:, hh, :], k.ident[:], ALU.add, en="pool")
                    cur = 0
                    for lvl in range(1, 7):
                        N, P, W = Nb[cur], Pb[cur], Wt[cur]
                        N2, P2, W2 = Nb[1 - cur], Pb[1 - cur], Wt[1 - cur]
                        last = (lvl == 6)
                        bP2 = k.bank(0, 8)
                        for hh in range(2):
                            fw.mm(bP2[:, hh * 128:(hh + 1) * 128], N[:, hh, :], P[:, hh, :])
                        fw.cp(P2[:, :, :], V(bP2.h[:, 0:256].rearrange("p (h c) -> p h c", h=2), bP2.res), en="act")
                        if not last:
                            bN2 = k.bank(0, 8)
                            for hh in range(2):
                                fw.mm(bN2[:, hh * 128:(hh + 1) * 128], P[:, hh, :], N[:, hh, :])
                            fw.cp(N2[:, :, :], V(bN2.h[:, 0:256].rearrange("p (h c) -> p h c", h=2), bN2.res))
                        bW = k.bank(0, 8)
                        for hh in range(2):
                            fw.mm(bW[:, hh * 128:(hh + 1) * 128], P2[:, hh, :], W[:, hh, :])
                        dstW = W2[:, :, :] if not last else ms[:, :, 0, :]
                        fw.tt(dstW, V(bW.h[:, 0:256].rearrange("p (h c) -> p h c", h=2), bW.res), W[:, :, :], ALU.add)
                        cur = 1 - cur
                    fw.dma(MATS[d][hp, tix, :, :], V(ms.h[:, :, :, :].rearrange("p h w c -> p (h w c)"), ms.res))
    with k.stage():
        At = fw.sb("At", [128, NT]); Rt = fw.sb("Rt", [128, NT])
        Ktm = fw.sb("Ktm", [128, 34, 128]); Btm = fw.sb("Btm", [128, 34, 128]); Vtm = fw.sb("Vtm", [128, 34, 128])
        cwl = fw.sb("cwl", [128, 34])
        mats = [fw.sb("mats", [128, 2, 4, 128]) for _ in range(3)]
        Stb = [fw.sb("St", [128, 64]) for _ in range(2)]
        Xs = [fw.sb("Xs", [128, 128]) for _ in range(2)]
        SAn = [fw.sb("SAn", [128, 128]) for _ in range(2)]
        yst = [fw.sb("yst", [128, 128]) for _ in range(3)]
        vsrc = VTM.h.rearrange("(n p) c -> p n c", p=128)
        mc = 0
        for d in range(2):
            order = list(range(34)) if d == 0 else [1, 0] + list(range(33, 1, -1))
            ksrc = KTM[d].h.rearrange("(n p) c -> p n c", p=128)
            bsrc = BTM[d].h.rearrange("(n p) c -> p n c", p=128)
            for (cs, kc, vc, yr, hp) in stiles:
                fw.dma(At[:], AT[d][cs, :]); fw.dma(Rt[:], RT[d][cs, :])
                fw.dma(Ktm[:], V(ksrc[:, :, kc], KTM[d].res)); fw.dma(Btm[:], V(bsrc[:, :, kc], BTM[d].res))
                fw.dma(Vtm[:], V(vsrc[:, :, vc], VTM.res))
                fw.dma(cwl[:], CWL[d][cs, :])
                St = Stb[0]
                fw.memset(St[:], 0.0)
                cur = 0
                for tix in order:
                    tk = slice(tix * 128, (tix + 1) * 128)
                    mt = mats[mc % 3]
                    fw.dma(V(mt.h[:, :, :, :].rearrange("p h w c -> p (h w c)"), mt.res), MATS[d][hp, tix, :, :])
                    St = Stb[cur]; St2 = Stb[1 - cur]
                    xs = Xs[mc % 2]; sa = SAn[mc % 2]; ys = yst[mc % 3]
                    mc += 1
                    bX = k.bank(0, 8)
                    for hh in range(2):
                        hs = slice(64 * hh, 64 * hh + 64)
                        fw.mm(bX[:, hs], At[hs, tk], St[hs, 0:64], start=True, stop=False)
                        fw.mm(bX[:, hs], mt[:, hh, 1, :], Vtm[:, tix, hs], start=False, stop=True)
                    fw.cp(xs[:], bX[:, 0:128])
                    bS = k.bank(0, 8)
                    for hh in range(2):
                        hs = slice(64 * hh, 64 * hh + 64)
                        fw.mm(bS[:, hs], mt[:, hh, 0, :], xs[:, hs])
                    fw.ts(sa[:], bS[:, 0:128], -1.0, None, ALU.mult)
                    bY = k.bank(0, 8)
                    for hh in range(2):
                        hs = slice(64 * hh, 64 * hh + 64)
                        fw.mm(bY[hs, 0:128], St[hs, 0:64], Rt[hs, tk], start=True, stop=False)
                        fw.mm(bY[hs, 0:128], sa[:, hs], mt[:, hh, 2, :], start=False, stop=False)
                        fw.mm(bY[hs, 0:128], Vtm[:, tix, hs], mt[:, hh, 3, :], start=False, stop=True)
                    fw.cp(ys[:], bY[:, 0:128], en="act")
                    fw.dma(YD[d][yr, tk], ys[:])
                    bT = k.bank(0, 8)
                    fw.mm(bT[:, 0:64], k.ident[:, :], St[:, 0:64], start=True, stop=False)
                    for hh in range(2):
                        hs = slice(64 * hh, 64 * hh + 64)
                        fw.mm(bT[hs, 0:64], Btm[:, tix, hs], sa[:, hs], start=False, stop=False)
                        fw.mm(bT[hs, 0:64], Ktm[:, tix, hs], Vtm[:, tix, hs], start=False, stop=True)
                    fw.ts(St2[:], bT[:, 0:64], cwl[:, tix:tix + 1], None, ALU.mult)
                    cur = 1 - cur


def stage_rwkv(k, I, fm, ofm):
    fw = k.fw
    S = k.scr
    RT = [S(f"rk_RT{d}", [512, NT]) for d in range(2)]
    AT = [S(f"rk_AT{d}", [512, NT]) for d in range(2)]
    KT = [S(f"rk_KT{d}", [512, NT]) for d in range(2)]
    BT = [S(f"rk_BT{d}", [512, NT]) for d in range(2)]
    KTM = [S(f"rk_KTM{d}", [NT, 512]) for d in range(2)]
    BTM = [S(f"rk_BTM{d}", [NT, 512]) for d in range(2)]
    VTM = S("rk_VTM", [NT, 512])
    G = S("rk_G", [512, NT])
    BON = S("rk_BON", [512, NT])
    CWL = [S(f"rk_CWL{d}", [512, 34]) for d in range(2)]
    MATS = [S(f"rk_MATS{d}", [4, 34, 128, 8 * 128]) for d in range(2)]
    YD = [S(f"rk_Y{d}", [512, NT]) for d in range(2)]
    NEGW = -_math.exp(-0.5)
    with k.stage():
        muT = fw.sb("muT", [128, 14])
        k.load_cols(muT[:, 0:14], I['l0_mu'], 14)
        colp = fw.sb("colp", [128, 8, 4])
        for i, n in enumerate(['l0_w0_f', 'l0_w0_b', 'l0_a0', 'l0_k_k', 'l0_k_a', 'l0_r_k']):
            k.load_cols(colp[:, i, :], I[n], 4)
        w2 = [fw.sb("w2f", [64, 512]), fw.sb("w2b", [64, 512])]
        fw.dma(w2[0][:], I['l0_w2_f'][:, :])
        fw.dma(w2[1][:], I['l0_w2_b'][:, :])
        a2t = fw.sb("a2t", [128, 512])
        fw.dma(a2t[64:128, :], I['l0_a2'][:, :])
        g2 = fw.sb("g2", [128, 512])
        fw.dma(g2[:], I['l0_g2'][:, :])
        Bones = fw.sb("Bones", [128, 128])
        fw.memset(Bones[:], 0.0)
        fw.memset(Bones[0:64, 0:64], 1.0)
        fw.memset(Bones[64:128, 64:128], 1.0)
        masks = fw.sb("masks", [128, 4, 128])
        fw.dma(masks[:], V(I['rk_masks'].h.rearrange("m p t -> p m t"), I['rk_masks'].res))
        Pt = [fw.sb("Pt", [128, 514]) for _ in range(2)]
        tmpb = [fw.sb("tmpb", [128, 512]) for _ in range(2)]
        Fs = [fw.sb("Fs", [128, 512]) for _ in range(14)]
        tw = fw.sb("tw", [64, 512])
        sg = fw.sb("sg", [128, 512])
        lw = [fw.sb("lw", [128, 512]) for _ in range(2)]
        a_ = fw.sb("a_", [128, 512])
        kk = fw.sb("kk", [128, 512])
        kmod = fw.sb("kmod", [128, 512])
        bb_ = fw.sb("bb_", [128, 512])
        t1 = fw.sb("t1", [128, 512])
        sq = fw.sb("sq", [128, 512])
        cw = fw.sb("cw", [128, 512])
        icw = fw.sb("icw", [128, 512])
        cwp = fw.sb("cwp", [128, 512])
        outs = [fw.sb("outs", [128, 512]) for _ in range(4)]
        tmst = [fw.sb("tmst", [128, 4, 128]) for _ in range(3)]
        lwT = [fw.sb("lwT", [128, 128]) for _ in range(2)]
        cwl = [fw.sb("cwl", [128, 4]) for _ in range(2)]
        oc = 0
        tc = 0

        def tm_out(src, dst_dram, t0, nt, cs):
            nonlocal tc
            st = tmst[tc % 3]
            tc += 1
            for ti in range(nt):
                b = k.bank(0, 6)
                fw.tr(b[:, 0:128], src[:, ti * 128:(ti + 1) * 128], k.ident[:])
                fw.cp(st[:, ti, :], b[:, 0:128], en=("act" if ti % 2 else "dve"))
            dv = dst_dram.h.rearrange("(n p) c -> p n c", p=128)
            fw.dma(V(dv[:, t0 // 128:t0 // 128 + nt, cs], dst_dram.res), st[:, 0:nt, :])

        for (t0, nb, which) in GROUPS512:
            seg0, seg1 = (0, 256) if which == 1 else (256, NT)
            nt = nb // 128
            for ci in range(14):
                P = Pt[ci % 2]
                lo = max(t0 - 1, seg0)
                hi = min(t0 + nb + 1, seg1)
                if t0 - 1 < seg0:
                    fw.memset(P[:, 0:1], 0.0)
                if t0 + nb + 1 > seg1:
                    fw.memset(P[:, nb + 1:nb + 2], 0.0)
                fw.dma(P[:, lo - (t0 - 1):hi - (t0 - 1)], fm[1024 + ci * 128:1024 + (ci + 1) * 128, lo:hi])
                tmp = tmpb[ci % 2]
                fw.tt(tmp[:, 0:nb], P[:, 0:nb], P[:, 2:nb + 2], ALU.add, en="pool")
                fw.stt(tmp[:, 0:nb], tmp[:, 0:nb], 0.5, P[:, 1:nb + 1], ALU.mult, ALU.subtract)
                fw.stt(Fs[ci][:, 0:nb], tmp[:, 0:nb], muT[:, ci:ci + 1], P[:, 1:nb + 1], ALU.mult, ALU.add)
            fw.act(tw[:, 0:nb], Fs[12][0:64, 0:nb], AF.Tanh)
            fw.act(sg[:, 0:nb], Fs[13][:, 0:nb], AF.Sigmoid)
            for hp in range(4):
                cs = slice(hp * 128, (hp + 1) * 128)
                r, kx, v = Fs[hp], Fs[4 + hp], Fs[8 + hp]
                n_ = slice(0, nb)
                for d in range(2):
                    b = k.bank(0, 6)
                    fw.mm(b[:, n_], w2[d][0:64, cs], tw[0:64, n_])
                    fw.act(lw[d][:, n_], b[:, n_], AF.Sigmoid, bias=colp[:, d, hp:hp + 1])
                    fw.ts(lw[d][:, n_], lw[d][:, n_], NEGW, None, ALU.mult)
                b = k.bank(0, 6)
                fw.mm(b[:, n_], a2t[64:128, cs], Fs[12][64:128, n_])
                fw.act(a_[:, n_], b[:, n_], AF.Sigmoid, bias=colp[:, 2, hp:hp + 1])
                b = k.bank(0, 6)
                fw.mm(b[:, n_], g2[:, cs], sg[:, n_])
                o = outs[oc % 4]; oc += 1
                fw.cp(o[:, n_], b[:, n_], en="act")
                fw.dma(G[cs, t0:t0 + nb], o[:, n_])
                fw.ts(kk[:, n_], kx[:, n_], colp[:, 3, hp:hp + 1], None, ALU.mult)
                fw.tt(sq[:, n_], kk[:, n_], kk[:, n_], ALU.mult, en="pool")
                b = k.bank(0, 6)
                fw.mm(b[:, n_], Bones[:, :], sq[:, n_])
                fw.ts(sq[:, n_], b[:, n_], 1e-24, None, ALU.max)
                fw.act(sq[:, n_], sq[:, n_], AF.Sqrt)
                fw.recip(sq[:, n_], sq[:, n_])
                fw.tt(kk[:, n_], kk[:, n_], sq[:, n_], ALU.mult)
                fw.ts(t1[:, n_], a_[:, n_], -1.0, colp[:, 4, hp:hp + 1], ALU.add, ALU.mult)
                fw.ts(t1[:, n_], t1[:, n_], 1.0, None, ALU.add)
                fw.tt(kmod[:, n_], kx[:, n_], t1[:, n_], ALU.mult)
                fw.tt(bb_[:, n_], kk[:, n_], a_[:, n_], ALU.mult, en="pool")
                fw.tt(t1[:, n_], r[:, n_], kmod[:, n_], ALU.mult, en="pool")
                fw.ts(t1[:, n_], t1[:, n_], colp[:, 5, hp:hp + 1], None, ALU.mult)
                b = k.bank(0, 6)
                fw.mm(b[:, n_], Bones[:, :], t1[:, n_])
                o = outs[oc % 4]; oc += 1
                fw.tt(o[:, n_], b[:, n_], v[:, n_], ALU.mult)
                fw.dma(BON[cs, t0:t0 + nb], o[:, n_])
                tm_out(v, VTM, t0, nt, cs)
                for d in range(2):
                    bi, bs = k.ps[6], k.ps[7]
                    for ti in range(nt):
                        b = k.bank(0, 6)
                        fw.tr(b[:, 0:128], lw[d][:, ti * 128:(ti + 1) * 128], k.ident[:])
                        lt = lwT[ti % 2]
                        fw.cp(lt[:], b[:, 0:128], en=("act" if ti % 2 else "dve"))
                        fw.mm(bi[:, ti * 128:(ti + 1) * 128], lt[:, :], masks[:, 2 * d, :])
                        fw.mm(bs[:, ti * 128:(ti + 1) * 128], lt[:, :], masks[:, 2 * d + 1, :])
                    fw.act(cw[:, n_], bi[:, n_], AF.Exp)
                    fw.act(icw[:, n_], bi[:, n_], AF.Exp, scale=-1.0)
                    fw.act(cwp[:, n_], bs[:, n_], AF.Exp)
                    for (a, bsrc, dstfm, dsttm, eng) in [(r, cw, RT[d], None, "dve"), (kmod, icw, KT[d], KTM[d], "pool"),
                                                         (bb_, icw, BT[d], BTM[d], "dve"), (kk, cwp, AT[d], None, "pool")]:
                        o = outs[oc % 4]; oc += 1
                        fw.tt(o[:, n_], a[:, n_], bsrc[:, n_], ALU.mult, en=eng)
                        fw.dma(dstfm[cs, t0:t0 + nb], o[:, n_])
                        if dsttm is not None:
                            tm_out(o, dsttm, t0, nt, cs)
                    cl = cwl[d]
                    off = 127 if d == 0 else 0
                    fw.cp(cl[:, 0:nt], V(cw.h[:, off:nb:128], cw.res))
                    fw.dma(CWL[d][cs, t0 // 128:t0 // 128 + nt], cl[:, 0:nt])
    _t = [slice(hp * 128, (hp + 1) * 128) for hp in range(4)]
    dplr_gpass_scan(k, I, [(_t[hp], hp) for hp in range(4)], [(_t[hp], _t[hp], _t[hp], _t[hp], hp) for hp in range(4)],
                    RT, AT, KT, BT, KTM, BTM, VTM, CWL, MATS, YD)
    with k.stage():
        colq = fw.sb("colq", [128, 2, 4])
        k.load_cols(colq[:, 0, :], I['l0_lnx_w'], 4)
        k.load_cols(colq[:, 1, :], I['l0_lnx_b'], 4)
        Bavg = fw.sb("Bavg", [128, 128])
        fw.memset(Bavg[:], 0.0)
        fw.memset(Bavg[0:64, 0:64], 1.0 / 64)
        fw.memset(Bavg[64:128, 64:128], 1.0 / 64)
        y0 = [fw.sb("y0", [128, 512]) for _ in range(2)]
        y1 = [fw.sb("y1", [128, 512]) for _ in range(2)]
        gg = [fw.sb("gg", [128, 512]) for _ in range(2)]
        bo = [fw.sb("bo", [128, 512]) for _ in range(2)]
        yc = [fw.sb("yc", [128, 512]) for _ in range(2)]
        sq2 = [fw.sb("sq2", [128, 512]) for _ in range(2)]
        i = 0
        for (t0, nb, which) in GROUPS512:
            for hp in range(4):
                cs = slice(hp * 128, (hp + 1) * 128)
                n_ = slice(0, nb)
                a, b_, g_, o_, c_, s_ = y0[i % 2], y1[i % 2], gg[i % 2], bo[i % 2], yc[i % 2], sq2[i % 2]
                i += 1
                fw.dma(a[:, n_], YD[0][cs, t0:t0 + nb]); fw.dma(b_[:, n_], YD[1][cs, t0:t0 + nb])
                fw.dma(g_[:, n_], G[cs, t0:t0 + nb]); fw.dma(o_[:, n_], BON[cs, t0:t0 + nb])
                fw.tt(a[:, n_], a[:, n_], b_[:, n_], ALU.add, en="pool")
                bm = k.bank(0, 8)
                fw.mm(bm[:, n_], Bavg[:, :], a[:, n_])
                fw.tt(c_[:, n_], a[:, n_], bm[:, n_], ALU.subtract)
                fw.tt(s_[:, n_], c_[:, n_], c_[:, n_], ALU.mult, en="pool")
                bv = k.bank(0, 8)
                fw.mm(bv[:, n_], Bavg[:, :], s_[:, n_])
                fw.ts(s_[:, n_], bv[:, n_], 64e-5, None, ALU.add)
                fw.act(s_[:, n_], s_[:, n_], AF.Sqrt)
                fw.recip(s_[:, n_], s_[:, n_])
                fw.tt(c_[:, n_], c_[:, n_], s_[:, n_], ALU.mult)
                fw.ts(c_[:, n_], c_[:, n_], colq[:, 0, hp:hp + 1], colq[:, 1, hp:hp + 1], ALU.mult, ALU.add)
                fw.tt(c_[:, n_], c_[:, n_], o_[:, n_], ALU.add, en="pool")
                fw.tt(c_[:, n_], c_[:, n_], g_[:, n_], ALU.mult)
                fw.dma(ofm[512 + hp * 128:512 + (hp + 1) * 128, t0:t0 + nb], c_[:, n_])
def stage_ssd(k, I, fm, tm, ofm):
    fw = k.fw
    S = k.scr
    XTM = S("sd_XTM", [NT, 512])
    BTM = S("sd_BTM", [NT, 256])
    BFM = S("sd_BFM", [256, NT])
    CFM = S("sd_CFM", [256, NT])
    YD = [S(f"sd_Y{d}", [NT, 512]) for d in range(2)]
    with k.stage():
        cwc = fw.sb("cwc", [128, 5, 8])
        for j in range(5):
            k.load_cols(cwc[:, j, :], I['l1_conv_w'], 8, off=j * 1024)
        cbc = fw.sb("cbc", [128, 8])
        k.load_cols(cbc[:, 0:8], I['l1_conv_b'], 8)
        Pt = [fw.sb("Pt", [128, 516]) for _ in range(2)]
        acc = [fw.sb("acc", [128, 512]) for _ in range(2)]
        outs = [fw.sb("outs", [128, 512]) for _ in range(3)]
        tmst = [fw.sb("tmst", [128, 4, 128]) for _ in range(2)]
        oc = 0
        for (t0, nb, which) in GROUPS512:
            seg0, seg1 = (0, 256) if which == 1 else (256, NT)
            nt = nb // 128
            n_ = slice(0, nb)
            for ci in range(8):
                P = Pt[ci % 2]
                lo = max(t0 - 2, seg0)
                hi = min(t0 + nb + 2, seg1)
                if t0 - 2 < seg0:
                    fw.memset(P[:, 0:2], 0.0)
                if t0 + nb + 2 > seg1:
                    fw.memset(P[:, nb + 2:nb + 4], 0.0)
                fw.dma(P[:, lo - (t0 - 2):hi - (t0 - 2)], fm[ci * 128:(ci + 1) * 128, lo:hi])
                a = acc[ci % 2]
                fw.ts(a[:, n_], P[:, 0:nb], cwc[:, 0, ci:ci + 1], None, ALU.mult)
                for j in range(1, 5):
                    fw.stt(a[:, n_], P[:, j:j + nb], cwc[:, j, ci:ci + 1], a[:, n_], ALU.mult, ALU.add)
                o = outs[oc % 3]; oc += 1
                fw.act(o[:, n_], a[:, n_], AF.Silu, bias=cbc[:, ci:ci + 1])
                if ci >= 4:
                    dst = BFM if ci < 6 else CFM
                    r0 = (ci - 4) * 128 if ci < 6 else (ci - 6) * 128
                    fw.dma(dst[r0:r0 + 128, t0:t0 + nb], o[:, n_])
                if ci < 6:
                    st = tmst[oc % 2]
                    for ti in range(nt):
                        b = k.bank(0, 8)
                        fw.tr(b[:, 0:128], o[:, ti * 128:(ti + 1) * 128], k.ident[:])
                        fw.cp(st[:, ti, :], b[:, 0:128], en=("act" if ti % 2 else "dve"))
                    dd = XTM if ci < 4 else BTM
                    c0 = ci * 128 if ci < 4 else (ci - 4) * 128
                    dv = dd.h.rearrange("(n p) c -> p n c", p=128)
                    fw.dma(V(dv[:, t0 // 128:t0 // 128 + nt, c0:c0 + 128], dd.res), st[:, 0:nt, :])
    with k.stage():
        masks = fw.sb("masks", [128, 4, 128])
        fw.dma(masks[:], V(I['rk_masks'].h.rearrange("m p t -> p m t"), I['rk_masks'].res))
        mbias = fw.sb("mbias", [128, 2, 128])
        for d in range(2):
            fw.ts(mbias[:, d, :], masks[:, 2 * d, :], -1.0, 1.0e5, ALU.add, ALU.mult)
        ones = fw.sb("ones", [128, 128])
        fw.memset(ones[:], 1.0)
        dtb = fw.sb("dtb", [128, 16])
        fw.dma(dtb[:, 0:8], V(I['l1_dt_bias_f'].h[0:8].partition_broadcast(128), I['l1_dt_bias_f'].res))
        fw.dma(dtb[:, 8:16], V(I['l1_dt_bias_b'].h[0:8].partition_broadcast(128), I['l1_dt_bias_b'].res))
        negA = fw.sb("negA", [128, 16])
        fw.dma(negA[:, 0:8], V(I['l1_a_log_f'].h[0:8].partition_broadcast(128), I['l1_a_log_f'].res))
        fw.dma(negA[:, 8:16], V(I['l1_a_log_b'].h[0:8].partition_broadcast(128), I['l1_a_log_b'].res))
        fw.act(negA[:], negA[:], AF.Exp)
        fw.ts(negA[:], negA[:], -1.0, None, ALU.mult)
        Bf = fw.sb("Bf", [128, 2, NT]); Cf = fw.sb("Cf", [128, 2, NT])
        for g in range(2):
            fw.dma(Bf[:, g, :], BFM[g * 128:(g + 1) * 128, :])
            fw.dma(Cf[:, g, :], CFM[g * 128:(g + 1) * 128, :])
        xs = [fw.sb("xs", [128, 512]) for _ in range(2)]
        btm = [fw.sb("btm", [128, 256]) for _ in range(2)]
        dtr = [fw.sb("dtr", [128, 16]) for _ in range(2)]
        dt = fw.sb("dt", [128, 8]); la = fw.sb("la", [128, 8])
        cumT = fw.sb("cumT", [128, 8]); ecT = fw.sb("ecT", [128, 8]); wend = fw.sb("wend", [128, 8]); dec = fw.sb("dec", [128, 8])
        lam = [fw.sb("lam", [128, 128]) for _ in range(2)]
        xe = [fw.sb("xe", [128, 128]) for _ in range(2)]
        WT = [fw.sb("WT", [128, 128]) for _ in range(3)]
        xdt = fw.sb("xdt", [128, 8, 64]); xw = fw.sb("xw", [128, 8, 64])
        hT = [fw.sb("hT", [128, 8, 64]) for _ in range(2)]
        tmpy = fw.sb("tmpy", [128, 8, 64])
        yo = [fw.sb("yo", [128, 512]) for _ in range(2)]
        dtsrc = tm
        cnt = 0
        for d in range(2):
            order = list(range(34)) if d == 0 else [1, 0] + list(range(33, 1, -1))
            fw.memset(hT[0][:], 0.0)
            cur = 0
            for tix in order:
                tk = slice(tix * 128, (tix + 1) * 128)
                x_ = xs[cnt % 2]; b_ = btm[cnt % 2]; dr = dtr[cnt % 2]; y_ = yo[cnt % 2]
                cnt += 1
                h0 = hT[cur]; h1 = hT[1 - cur]
                fw.dma(x_[:], XTM[tk, :]); fw.dma(b_[:], BTM[tk, :]); fw.dma(dr[:], tm[tk, 512:528])
                fw.tt(dt[:], dr[:, d * 8:(d + 1) * 8], dtb[:, d * 8:(d + 1) * 8], ALU.add)
                fw.act(dt[:], dt[:], AF.Exp)
                fw.act(dt[:], dt[:], AF.Ln, bias=1.0)
                fw.tt(la[:], dt[:], negA[:, d * 8:(d + 1) * 8], ALU.mult)
                bc = k.ps[0]
                fw.mm(bc[:, 0:8], masks[:, 2 * d, :], la[:, :])
                fw.mm(bc[:, 8:16], ones[:, :], la[:, :])
                fw.cp(cumT[:], bc[:, 0:8])
                fw.act(ecT[:], bc[:, 0:8], AF.Exp)
                fw.tt(wend[:], bc[:, 8:16], cumT[:], ALU.subtract)
                fw.act(wend[:], wend[:], AF.Exp)
                fw.act(dec[:], bc[:, 8:16], AF.Exp)
                fw.tt(xdt[:, :, :], V(x_.h[:, :].rearrange("p (h c) -> p h c", h=8), x_.res),
                      V(dt.h[:, :].unsqueeze(2).to_broadcast([128, 8, 64]), dt.res), ALU.mult)
                fw.tt(xw[:, :, :], xdt[:, :, :], V(wend.h[:, :].unsqueeze(2).to_broadcast([128, 8, 64]), wend.res),
                      ALU.mult, en="pool")
                bcb = k.ps[1]
                for g in range(2):
                    fw.mm(bcb[:, g * 128:(g + 1) * 128], Bf[:, g, tk], Cf[:, g, tk])
                bY = k.ps[2]
                bU = k.ps[3]
                bH = k.ps[4]
                for h in range(8):
                    g = h // 4
                    lm = lam[h % 2]; e_ = xe[h % 2]; w_ = WT[h % 3]
                    fw.ts(lm[:], masks[:, 2 * d, :], la[:, h:h + 1], None, ALU.mult, en="pool")
                    bb = k.bank(5, 3)
                    fw.mm(bb[:, 0:128], ones[:, :], lm[:, :])
                    fw.stt(e_[:], bb[:, 0:128], cumT[:, h:h + 1], mbias[:, d, :], ALU.subtract, ALU.add)
                    fw.act(e_[:], e_[:], AF.Exp)
                    fw.tt(w_[:], bcb[:, g * 128:(g + 1) * 128], e_[:], ALU.mult)
                    if tix >= 2:
                        fw.mm(bY[:, h * 64:(h + 1) * 64], w_[:, :], xdt[:, h, :])
                        fw.mm(bU[:, h * 64:(h + 1) * 64], Cf[:, g, tk], h0[:, h, :])
                    fw.mm(bH[:, h * 64:(h + 1) * 64], b_[:, g * 128:(g + 1) * 128], xw[:, h, :])
                if tix >= 2:
                    fw.tt(tmpy[:, :, :], V(bU.h[:, :].rearrange("p (h c) -> p h c", h=8), bU.res),
                          V(ecT.h[:, :].unsqueeze(2).to_broadcast([128, 8, 64]), ecT.res), ALU.mult)
                    fw.tt(y_[:], bY[:, :], V(tmpy.h[:, :, :].rearrange("p h c -> p (h c)"), tmpy.res), ALU.add)
                    fw.dma(YD[d][tk, :], y_[:])
                fw.tt(h1[:, :, :], h0[:, :, :], V(dec.h[:, :].unsqueeze(2).to_broadcast([128, 8, 64]), dec.res),
                      ALU.mult, en="pool")
                fw.tt(h1[:, :, :], V(bH.h[:, :].rearrange("p (h c) -> p h c", h=8), bH.res), h1[:, :, :], ALU.add)
                cur = 1 - cur
    with k.stage():
        dsk = fw.sb("dsk", [128, 8])
        fw.dma(dsk[:], V(I['l1_d_skip'].h[0:8].partition_broadcast(128), I['l1_d_skip'].res))
        ngb = fw.sb("ngb", [128, 512])
        fw.dma(ngb[:], V(I['l1_ssm_norm'].h[0:512].partition_broadcast(128), I['l1_ssm_norm'].res))
        y0 = [fw.sb("y0", [128, 512]) for _ in range(2)]
        y1 = [fw.sb("y1", [128, 512]) for _ in range(2)]
        xx = [fw.sb("xx", [128, 512]) for _ in range(2)]
        zz = [fw.sb("zz", [128, 512]) for _ in range(2)]
        sq = [fw.sb("sq", [128, 512]) for _ in range(2)]
        ss = [fw.sb("ss", [128, 2]) for _ in range(2)]
        fst = [fw.sb("fst", [128, 4, 128]) for _ in range(2)]
        for i, tix in enumerate(range(2, 34)):
            tk = slice(tix * 128, (tix + 1) * 128)
            a, b_, x_, z_, s_, r_, f_ = y0[i % 2], y1[i % 2], xx[i % 2], zz[i % 2], sq[i % 2], ss[i % 2], fst[i % 2]
            fw.dma(a[:], YD[0][tk, :]); fw.dma(b_[:], YD[1][tk, :]); fw.dma(x_[:], XTM[tk, :]); fw.dma(z_[:], tm[tk, 0:512])
            fw.tt(a[:], a[:], b_[:], ALU.add, en="pool")
            fw.tt(V(x_.h[:, :].rearrange("p (h c) -> p h c", h=8), x_.res), V(x_.h[:, :].rearrange("p (h c) -> p h c", h=8), x_.res),
                  V(dsk.h[:, :].unsqueeze(2).to_broadcast([128, 8, 64]), dsk.res), ALU.mult)
            fw.tt(a[:], a[:], x_[:], ALU.add, en="pool")
            fw.act(z_[:], z_[:], AF.Silu)
            fw.tt(a[:], a[:], z_[:], ALU.mult)
            fw.tt(s_[:], a[:], a[:], ALU.mult, en="pool")
            fw.red(r_[:], V(s_.h[:, :].rearrange("p (g c) -> p g c", g=2), s_.res), ALU.add)
            fw.ts(r_[:], r_[:], 1.0 / 256, EPS, ALU.mult, ALU.add)
            fw.act(r_[:], r_[:], AF.Sqrt)
            fw.recip(r_[:], r_[:])
            fw.tt(V(a.h[:, :].rearrange("p (g c) -> p g c", g=2), a.res), V(a.h[:, :].rearrange("p (g c) -> p g c", g=2), a.res),
                  V(r_.h[:, :].unsqueeze(2).to_broadcast([128, 2, 256]), r_.res), ALU.mult)
            fw.tt(a[:], a[:], ngb[:], ALU.mult, en="pool")
            for c in range(4):
                b = k.bank(0, 8)
                fw.tr(b[:, 0:128], a[:, c * 128:(c + 1) * 128], k.ident[:])
                fw.cp(f_[:, c, :], b[:, 0:128], en=("act" if c % 2 else "dve"))
            dv = ofm.h.rearrange("(c p) t -> p c t", p=128)
            fw.dma(V(dv[:, 0:4, tk], ofm.res), f_[:, :, :])
L0_FM = [(i * 128, 128, i * 128) for i in range(8)] + [(1536 + i * 128, 128, 1024 + i * 128) for i in range(14)]
L0_TM = [(1024, 512, 0)]
L1_FM = ([(512 + i * 128, 128, i * 128) for i in range(8)] + [(1552 + i * 128, 128, 1024 + i * 128) for i in range(4)]
         + [(2576 + i * 128, 128, 1536 + i * 128) for i in range(4)] + [(3088, 16, 2048)])
L1_TM = [(0, 512, 0), (1536, 16, 512), (2064, 512, 528)]
ALL_STAGES = ('mod0', 'in0', 'attn', 'rwkv', 'out0', 'mod1', 'in1', 'ssd', 'gla', 'out1')


def build_program(shapes, stages=ALL_STAGES, debug_outs=(), ext_in=()):
    k = K(debug_outs)
    fw = k.fw
    I = {n: k.din(n, s) for n, s in shapes.items()}
    out = T(fw, "out", [NLAT, D], F32, "dram", "ExternalOutput")
    k.dram["out"] = out

    def scr(name, shape):
        if name in ext_in:
            return I[name]
        return k.dscratch(name, shape)
    xres0 = scr("xres0", [NT, D])
    xmid = scr("xmid", [NT, D])
    fm = scr("fm", [2816, NT])
    tm = scr("tm", [NT, 1040])
    ofm = scr("ofm", [1024, NT])
    k.scr = scr
    ls0 = ExitStack()
    if 'mod0' in stages:
        modA, modB, Gbc = k.stage_mod(0, I['cvec'], I['l0_mod_w'], I['l0_mod_b'], I['l0_norm1'], I['l0_norm2'], ls0)
    if 'in0' in stages:
        k.stage_inproj(I['xin'], I['l0_w_in'], 3328, modA, modB, L0_FM, L0_TM, fm, tm)
    if 'attn' in stages:
        stage_attention(k, I, fm, tm, ofm)
    if 'rwkv' in stages:
        stage_rwkv(k, I, fm, ofm)
    if 'out0' in stages:
        k.stage_outproj(ofm, I['xin'], xmid, I['l0_w_out'], Gbc, list(range(34)))
        k.stage_mlp(xmid, xres0, I['l0_mlp_w1'], I['l0_mlp_w2'], modA, modB, Gbc, GROUPS256)
    fw.barrier()
    ls0.close()
    ls1 = ExitStack()
    if 'mod1' in stages:
        modA1, modB1, Gbc1 = k.stage_mod(1, I['cvec'], I['l1_mod_w'], I['l1_mod_b'], I['l1_norm1'], I['l1_norm2'], ls1)
    if 'in1' in stages:
        k.stage_inproj(xres0, I['l1_w_in'], 3104, modA1, modB1, L1_FM, L1_TM, fm, tm)
    if 'ssd' in stages:
        stage_ssd(k, I, fm, tm, ofm)
    if 'gla' in stages:
        stage_gla(k, I, fm, tm, ofm)
    if 'out1' in stages:
        k.stage_outproj(ofm, xres0, xmid, I['l1_w_out'], Gbc1, list(range(2, 34)))
        k.stage_mlp(xmid, None, I['l1_mlp_w1'], I['l1_mlp_w2'], modA1, modB1, Gbc1, GROUPS256[1:],
                    final_norm=I['final_norm'], out_t=out)
    fw.barrier()
    ls1.close()
    return k
def _consts():
    c = {}
    t = np.arange(NLAT)
    row = (t // 64).astype(np.float32)
    col = (t % 64).astype(np.float32)
    nf = 16
    inv = (np.float32(10000.0) ** (-np.arange(nf, dtype=np.float32) / nf)).astype(np.float32)
    ar = row[None, :] * inv[:, None]
    ac = col[None, :] * inv[:, None]
    C = np.concatenate([np.cos(ar), np.cos(ar), np.cos(ac), np.cos(ac)], 0).astype(np.float32)
    S = np.concatenate([np.sin(ar), np.sin(ar), np.sin(ac), np.sin(ac)], 0).astype(np.float32)
    c['rope_cos'] = C
    c['rope_sin'] = S
    Pm = np.zeros((64, 64), np.float32)
    for base in (0, 32):
        for d in range(16):
            Pm[base + d, base + d + 16] = -1.0
            Pm[base + d + 16, base + d] = 1.0
    c['rope_pT'] = np.ascontiguousarray(Pm.T)
    s_ = np.arange(128)[:, None]
    t_ = np.arange(128)[None, :]
    c['rk_masks'] = np.stack([(s_ <= t_), (s_ < t_), (s_ >= t_), (s_ > t_)], 0).astype(np.float32)
    return c


_CACHE = {}


def _prep_inputs(inputs, b):
    m = {}
    m['xin'] = np.ascontiguousarray(np.concatenate([inputs['ctx'][b], inputs['x'][b]], 0))
    cv = np.stack([inputs['c'][b].reshape(8, 128), inputs['c_ctx'].reshape(8, 128)], 1).reshape(16, 128)
    m['cvec'] = np.ascontiguousarray(cv.reshape(-1))
    for n, v in inputs.items():
        if n in ('x', 'c', 'ctx', 'c_ctx'):
            continue
        m[n] = np.ascontiguousarray(v)
    m.update(_consts())
    return m


def run(inputs, stages=ALL_STAGES, debug_outs=(), extra=None, trace=False, ncores=8, only=None, ext_in=()):
    inputs = {n: np.asarray(v, dtype=np.float32) for n, v in inputs.items()}
    maps = []
    for b in range(ncores):
        m = _prep_inputs(inputs, b)
        if only is not None:
            m = {n: v for n, v in m.items() if n in only}
        if extra:
            m.update(extra(b))
        maps.append(m)
    shapes = {n: tuple(v.shape) for n, v in maps[0].items()}
    key = (tuple(stages), tuple(sorted(debug_outs)), tuple(sorted(shapes.items())), tuple(sorted(ext_in)))
    if key not in _CACHE:
        _CACHE[key] = build_program(shapes, stages, debug_outs, ext_in)
    k = _CACHE[key]
    res = run_bass_kernel_spmd(k.nc, maps, core_ids=list(range(ncores)), trace=trace)
    return res


def kernel(**inputs):
    res = run(inputs)
    return np.stack([np.asarray(r["out"], dtype=np.float32) for r in res.results], 0)
```
